# Optimizing a Trainium2 kernel written in Bass

```python
import jax, jax.numpy as jnp
from jax import lax
import numpy as np

D_MODEL = 2048
BATCH = 16
SEQ = 2048
DEPTH = 1
DEC_BATCH = 32
DEC_SEQ = 32
PAST_LEN = 1024

CHUNK = 64
POOL_WINDOWS = (2, 4, 8, 16)
N_POOL_GROUPS = 4
POOL_WIDTH = D_MODEL // 2
POOL_GROUP = POOL_WIDTH // N_POOL_GROUPS
POOL_STATE = max(POOL_WINDOWS) - 1
MLSTM_HEADS = 8
MLSTM_WIDTH = D_MODEL
MLSTM_HEAD_DIM = MLSTM_WIDTH // MLSTM_HEADS
D_FF = -(-8 * D_MODEL // (3 * 256)) * 256
ALPHA = (2 * DEPTH) ** 0.25
BETA = (8 * DEPTH) ** -0.25
LN_EPS = 1e-5
SPLIT_IDX = [POOL_WIDTH,
             POOL_WIDTH + MLSTM_WIDTH,
             POOL_WIDTH + 2 * MLSTM_WIDTH,
             POOL_WIDTH + 3 * MLSTM_WIDTH,
             POOL_WIDTH + 4 * MLSTM_WIDTH,
             POOL_WIDTH + 4 * MLSTM_WIDTH + MLSTM_HEADS,
             POOL_WIDTH + 4 * MLSTM_WIDTH + 2 * MLSTM_HEADS,
             POOL_WIDTH + 4 * MLSTM_WIDTH + 2 * MLSTM_HEADS + D_MODEL]
N_IN = POOL_WIDTH + 4 * MLSTM_WIDTH + 2 * MLSTM_HEADS + 2 * D_MODEL

kernel_name = "pool_mlstm_gated_deepnorm_adaln_stream_step"


def layer_norm(x, g=None, b=None):
    xf = x.astype(jnp.float32)
    mu = jnp.mean(xf, axis=-1, keepdims=True)
    var = jnp.mean(jnp.square(xf - mu), axis=-1, keepdims=True)
    y = (xf - mu) * lax.rsqrt(var + LN_EPS)
    if g is not None:
        y = y * g.astype(jnp.float32) + b.astype(jnp.float32)
    return y.astype(x.dtype)


def pool_mixer(p, prefix, start_pos, w_pool, pool_scale):
    B, L, _ = p.shape
    ext = jnp.concatenate([prefix.astype(p.dtype), p], axis=1)
    ef = ext.astype(jnp.float32)
    cs = jnp.concatenate([jnp.zeros_like(ef[:, :1]), jnp.cumsum(ef, axis=1)], axis=1)
    pos = start_pos + jnp.arange(L)
    hi = cs[:, POOL_STATE + 1:POOL_STATE + 1 + L]
    tok = ef[:, POOL_STATE:]
    outs = []
    for g, w in enumerate(POOL_WINDOWS):
        sl = slice(g * POOL_GROUP, (g + 1) * POOL_GROUP)
        lo = cs[:, POOL_STATE + 1 - w:POOL_STATE + 1 - w + L, sl]
        cnt = jnp.minimum(pos + 1, w).astype(jnp.float32)[None, :, None]
        outs.append((hi[..., sl] - lo) / cnt - tok[..., sl])
    y = jnp.stack(outs, axis=2).astype(p.dtype)
    y = jnp.einsum('blgc,gcd->blgd', y, w_pool).reshape(B, L, POOL_WIDTH) * pool_scale
    return y, ext[:, -POOL_STATE:]


def mlstm_chunk(carry, inp):
    C, n, m = carry
    q, k, v, ig, lf = inp
    L = q.shape[2]
    b = jnp.cumsum(lf, axis=-1)
    a = b + m[..., None]
    causal = jnp.tril(jnp.ones((L, L), dtype=bool))
    log_d = jnp.where(causal, b[..., :, None] - b[..., None, :] + ig[..., None, :], -jnp.inf)
    m_row = jnp.maximum(a, jnp.max(log_d, axis=-1))
    d = jnp.exp(log_d - m_row[..., None])
    w_inter = jnp.exp(a - m_row)
    s = jnp.einsum('bhld,bhsd->bhls', q, k) * d
    num = w_inter[..., None] * jnp.einsum('bhld,bhde->bhle', q, C) + jnp.einsum('bhls,bhse->bhle', s, v)
    den = w_inter * jnp.einsum('bhld,bhd->bhl', q, n) + jnp.sum(s, axis=-1)
    h = num / jnp.maximum(jnp.abs(den), jnp.exp(-m_row))[..., None]
    m_new = m_row[..., -1]
    w_s = jnp.exp(b[..., -1:] - b + ig - m_new[..., None])
    decay = jnp.exp(b[..., -1] + m - m_new)
    kw = k * w_s[..., None]
    C_new = decay[..., None, None] * C + jnp.einsum('bhsd,bhse->bhde', kw, v)
    n_new = decay[..., None] * n + jnp.sum(kw, axis=2)
    return (C_new, n_new, m_new), h


def mlstm_seq(q, k, v, ig, lf, C, n, m):
    L = q.shape[2]
    if L <= CHUNK:
        (C, n, m), h = mlstm_chunk((C, n, m), (q, k, v, ig, lf))
        return h, C, n, m
    nc = L // CHUNK
    def split(t):
        return jnp.moveaxis(t.reshape(t.shape[:2] + (nc, CHUNK) + t.shape[3:]), 2, 0)
    (C, n, m), h = lax.scan(mlstm_chunk, (C, n, m), (split(q), split(k), split(v), split(ig), split(lf)))
    h = jnp.moveaxis(h, 0, 2).reshape(q.shape[:2] + (L, q.shape[-1]))
    return h, C, n, m


def token_mixer(u, pool_prefix, C0, n0, m0, start_pos, w_in, b_i, b_f, w_pool, pool_scale, gn_w, w_pa, w_pb, w_out):
    B, L, _ = u.shape
    f32 = jnp.float32
    proj = u @ w_in
    p, q, k, v, o, ig, fg, ga, gb = jnp.split(proj, SPLIT_IDX, axis=-1)
    a_out, pool_state = pool_mixer(p, pool_prefix, start_pos, w_pool, pool_scale)
    def heads(t):
        return jnp.swapaxes(t.reshape(B, L, MLSTM_HEADS, MLSTM_HEAD_DIM).astype(f32), 1, 2)
    ig_t = jnp.swapaxes((ig + b_i).astype(f32), 1, 2)
    lf_t = jax.nn.log_sigmoid(jnp.swapaxes((fg + b_f).astype(f32), 1, 2))
    h, C, n, m = mlstm_seq(heads(q), heads(k) * (MLSTM_HEAD_DIM ** -0.5), heads(v), ig_t, lf_t,
                           C0.astype(f32), n0.astype(f32), m0.astype(f32))
    h = jnp.swapaxes(h, 1, 2)
    h = h * jax.nn.sigmoid(o.astype(f32)).reshape(B, L, MLSTM_HEADS, MLSTM_HEAD_DIM)
    mu = jnp.mean(h, axis=-1, keepdims=True)
    var = jnp.mean(jnp.square(h - mu), axis=-1, keepdims=True)
    h = (h - mu) * lax.rsqrt(var + LN_EPS) * gn_w.astype(f32).reshape(MLSTM_HEADS, MLSTM_HEAD_DIM)
    b_out = h.reshape(B, L, MLSTM_WIDTH).astype(u.dtype)
    merged = jax.nn.sigmoid(ga) * (a_out @ w_pa) + jax.nn.sigmoid(gb) * (b_out @ w_pb)
    return merged @ w_out, pool_state, C.astype(u.dtype), n.astype(u.dtype), m.astype(u.dtype)


def layer(x, c, pool_prefix, C0, n0, m0, start_pos, w_ada, b_ada, w_in, b_i, b_f, w_pool, pool_scale, gn_w,
          w_pa, w_pb, w_out, ln1_g, ln1_b, w_gate, w_up, w_down, ln2_g, ln2_b):
    mod = jax.nn.silu(c) @ w_ada + b_ada
    sh1, sc1, g1, sh2, sc2, g2 = [t[:, None, :] for t in jnp.split(mod, 6, axis=-1)]
    u = layer_norm(x) * (1 + sc1) + sh1
    t, pool_state, C, n, m = token_mixer(u, pool_prefix, C0, n0, m0, start_pos, w_in, b_i, b_f, w_pool,
                                         pool_scale, gn_w, w_pa, w_pb, w_out)
    x = layer_norm(ALPHA * x + g1 * t, ln1_g, ln1_b)
    u = layer_norm(x) * (1 + sc2) + sh2
    f = (jax.nn.silu(u @ w_gate) * (u @ w_up)) @ w_down
    x = layer_norm(ALPHA * x + g2 * f, ln2_g, ln2_b)
    return x, pool_state, C, n, m


def setup_inputs(seed: int = 0) -> dict:
    key = jax.random.key(seed)
    ks = jax.random.split(key, 32)
    f32 = jnp.float32
    def nrm(k, shape, s):
        return jax.random.normal(k, shape, f32) * s
    return {
        "x_prompt": nrm(ks[0], (BATCH, SEQ, D_MODEL), 1.0),
        "x_sample": nrm(ks[1], (DEC_BATCH, DEC_SEQ, D_MODEL), 1.0),
        "c_prompt": nrm(ks[2], (BATCH, D_MODEL), 1.0),
        "c_sample": nrm(ks[3], (DEC_BATCH, D_MODEL), 1.0),
        "state_pool": nrm(ks[4], (DEPTH, DEC_BATCH, POOL_STATE, POOL_WIDTH), 1.0),
        "state_mlstm_C": nrm(ks[5], (DEPTH, DEC_BATCH, MLSTM_HEADS, MLSTM_HEAD_DIM, MLSTM_HEAD_DIM), 0.05),
        "state_mlstm_n": nrm(ks[6], (DEPTH, DEC_BATCH, MLSTM_HEADS, MLSTM_HEAD_DIM), 0.05),
        "state_mlstm_m": nrm(ks[7], (DEPTH, DEC_BATCH, MLSTM_HEADS), 0.5),
        "w_ada": nrm(ks[8], (DEPTH, D_MODEL, 6 * D_MODEL), 0.5 * D_MODEL ** -0.5),
        "b_ada": nrm(ks[9], (DEPTH, 6 * D_MODEL), 0.02),
        "w_in": nrm(ks[10], (DEPTH, D_MODEL, N_IN), D_MODEL ** -0.5),
        "b_i": nrm(ks[11], (DEPTH, MLSTM_HEADS), 0.1),
        "b_f": jnp.linspace(3.0, 6.0, MLSTM_HEADS, dtype=f32)[None, :] + nrm(ks[12], (DEPTH, MLSTM_HEADS), 0.01),
        "w_pool": nrm(ks[13], (DEPTH, N_POOL_GROUPS, POOL_GROUP, POOL_GROUP), POOL_GROUP ** -0.5),
        "pool_scale": 1.0 + nrm(ks[14], (DEPTH, POOL_WIDTH), 0.02),
        "gn_w": 1.0 + nrm(ks[15], (DEPTH, MLSTM_WIDTH), 0.02),
        "w_pa": nrm(ks[16], (DEPTH, POOL_WIDTH, D_MODEL), POOL_WIDTH ** -0.5),
        "w_pb": nrm(ks[17], (DEPTH, MLSTM_WIDTH, D_MODEL), MLSTM_WIDTH ** -0.5),
        "w_out": nrm(ks[18], (DEPTH, D_MODEL, D_MODEL), BETA * D_MODEL ** -0.5),
        "ln1_g": 1.0 + nrm(ks[19], (DEPTH, D_MODEL), 0.02),
        "ln1_b": nrm(ks[20], (DEPTH, D_MODEL), 0.02),
        "w_gate": nrm(ks[21], (DEPTH, D_MODEL, D_FF), D_MODEL ** -0.5),
        "w_up": nrm(ks[22], (DEPTH, D_MODEL, D_FF), D_MODEL ** -0.5),
        "w_down": nrm(ks[23], (DEPTH, D_FF, D_MODEL), BETA * D_FF ** -0.5),
        "ln2_g": 1.0 + nrm(ks[24], (DEPTH, D_MODEL), 0.02),
        "ln2_b": nrm(ks[25], (DEPTH, D_MODEL), 0.02),
    }


def reference(x_prompt, x_sample, c_prompt, c_sample, state_pool, state_mlstm_C, state_mlstm_n, state_mlstm_m,
              w_ada, b_ada, w_in, b_i, b_f, w_pool, pool_scale, gn_w, w_pa, w_pb, w_out, ln1_g, ln1_b,
              w_gate, w_up, w_down, ln2_g, ln2_b):
    B = x_prompt.shape[0]
    y_prompt, y_sample = x_prompt, x_sample
    pool_p_l, C_p_l, n_p_l, m_p_l = [], [], [], []
    pool_s_l, C_s_l, n_s_l, m_s_l = [], [], [], []
    for l in range(DEPTH):
        wl = (w_ada[l], b_ada[l], w_in[l], b_i[l], b_f[l], w_pool[l], pool_scale[l], gn_w[l], w_pa[l], w_pb[l],
              w_out[l], ln1_g[l], ln1_b[l], w_gate[l], w_up[l], w_down[l], ln2_g[l], ln2_b[l])
        prefix0 = jnp.zeros((B, POOL_STATE, POOL_WIDTH), x_prompt.dtype)
        C0 = jnp.zeros((B, MLSTM_HEADS, MLSTM_HEAD_DIM, MLSTM_HEAD_DIM), jnp.float32)
        n0 = jnp.zeros((B, MLSTM_HEADS, MLSTM_HEAD_DIM), jnp.float32)
        m0 = jnp.zeros((B, MLSTM_HEADS), jnp.float32)
        y_prompt, ps, Cp, npp, mp = layer(y_prompt, c_prompt, prefix0, C0, n0, m0, 0, *wl)
        y_sample, ss, Cs, ns, ms = layer(y_sample, c_sample, state_pool[l], state_mlstm_C[l], state_mlstm_n[l],
                                         state_mlstm_m[l], PAST_LEN, *wl)
        pool_p_l.append(ps); C_p_l.append(Cp); n_p_l.append(npp); m_p_l.append(mp)
        pool_s_l.append(ss); C_s_l.append(Cs); n_s_l.append(ns); m_s_l.append(ms)
    pool_p = jnp.stack(pool_p_l, axis=0)
    C_p = jnp.stack(C_p_l, axis=0)
    n_p = jnp.stack(n_p_l, axis=0)
    m_p = jnp.stack(m_p_l, axis=0)
    pool_s = jnp.stack(pool_s_l, axis=0)
    C_s = jnp.stack(C_s_l, axis=0)
    n_s = jnp.stack(n_s_l, axis=0)
    m_s = jnp.stack(m_s_l, axis=0)
    return (y_prompt, y_sample, pool_p, C_p, n_p, m_p, pool_s, C_s, n_s, m_s)
```

```python
import numpy as np
from contextlib import ExitStack
import concourse.bass as bass
import concourse.mybir as mybir
from concourse.bass_utils import run_bass_kernel_spmd

F32 = mybir.dt.float32
BF16 = mybir.dt.bfloat16
AF = mybir.ActivationFunctionType
ALU = mybir.AluOpType
AX = mybir.AxisListType

D = 2048
NIN = 13328
DFF = 5632
NH = 8
HD = 256
PW = 1024
SEQ = 2048
NPS = 2
NSS = 4
SL = 32
ALPHA = float(2.0 ** 0.25)
EPS = 1e-5
NCH = 4
LP = 128
TMAX = NCH * LP
NFB = 4
FBK = 11
NSLOT = 3
SLOTB = 8192
COL_IG = 9216
COL_GA = 9232
COL_GB = 11280
DEBUG = {}
TILE_SEL = None
STOP_AT = None
STOP_ALL = False
LNCUT = 9
EVAC = 'mix'


def _esz(dt):
    return 4 if dt == F32 else 2


class Prog:
    NDMA = 24

    def __init__(self, nc):
        self.nc = nc
        self.ops = []
        self.W = {}
        self.R = {}

    def region(self, ap):
        t = ap.tensor
        name = t.name
        esz = _esz(ap.dtype)
        dims = list(ap.ap)
        off = ap.offset
        kind = type(t).__name__
        if "PSum" in kind:
            return (name, 0, 128, 0, 1 << 40)
        if "SB" in kind:
            rowb = _esz(t.dtype)
            for s in list(t.shape)[1:]:
                rowb *= s
            rowlen = rowb // esz
            p0 = off // rowlen
            col = off % rowlen
            pstep, pcnt = dims[0]
            npart = pcnt if pstep != 0 else 1
            ext = 1
            for st, cn in dims[1:]:
                ext += (cn - 1) * abs(st)
            return (name, p0, p0 + npart, col * esz, (col + ext) * esz)
        ext = 1
        for st, cn in dims:
            ext += (cn - 1) * abs(st)
        return (name, 0, 1, off * esz, (off + ext) * esz)

    @staticmethod
    def _ov(a, b):
        return a[1] < b[2] and b[1] < a[2] and a[3] < b[4] and b[3] < a[4]

    @staticmethod
    def _cov(a, b):
        return b[1] <= a[1] and a[2] <= b[2] and b[3] <= a[3] and a[4] <= b[4]

    def _buckets(self, r):
        name = r[0]
        if name.startswith("ps"):
            return [(name, 0)]
        bs = 1024 if name == "S" else (1 << 18)
        return [(name, b) for b in range(r[3] // bs, (r[4] - 1) // bs + 1)]

    def add(self, stream, fn, reads, writes, dma=False):
        idx = len(self.ops)
        deps = {}
        rr = [self.region(a) for a in reads if a is not None and not isinstance(a, (int, float))]
        ww = [self.region(a) for a in writes]
        ww += [r for r in rr if r[0].startswith("ps")]
        for r in rr:
            for bk in self._buckets(r):
                for (reg, o) in self.W.get(bk, ()):
                    if self._ov(r, reg):
                        deps[o] = True
        for w in ww:
            for bk in self._buckets(w):
                for (reg, o) in self.W.get(bk, ()):
                    if self._ov(w, reg):
                        deps.setdefault(o, False)
                for (reg, o) in self.R.get(bk, ()):
                    if self._ov(w, reg):
                        deps.setdefault(o, False)
        for w in ww:
            for bk in self._buckets(w):
                self.W[bk] = [(reg, o) for (reg, o) in self.W.get(bk, ()) if not self._cov(reg, w)] + [(w, idx)]
                self.R[bk] = [(reg, o) for (reg, o) in self.R.get(bk, ()) if not self._cov(reg, w)]
        for r in rr:
            for bk in self._buckets(r):
                lst = self.R.setdefault(bk, [])
                keep = []
                for (reg, o) in lst:
                    if o == idx:
                        if reg != r:
                            keep.append((reg, o))
                        continue
                    oo = self.ops[o]
                    if reg == r and oo["stream"] == stream and (not oo["dma"]) and (not dma):
                        continue
                    keep.append((reg, o))
                keep.append((r, idx))
                lst[:] = keep
        deps.pop(idx, None)
        self.ops.append(dict(stream=stream, fn=fn, deps=deps, dma=dma, signal=False))
        return idx

    def emit(self, es):
        nc = self.nc
        ops = self.ops
        streams = ["pe", "act", "dve", "pool", "sp"]
        need = []
        for i, op in enumerate(ops):
            nl = []
            for d, raw in op["deps"].items():
                dop = ops[d]
                if dop["dma"] or op["dma"]:
                    nl.append(d)
                elif dop["stream"] == op["stream"]:
                    if op["stream"] != "pe":
                        nl.append(d)
                else:
                    nl.append(d)
            for d in nl:
                ops[d]["signal"] = True
            need.append(nl)
        sems = {s: es.enter_context(nc.semaphore("sem_" + s)) for s in streams}
        dsems = [es.enter_context(nc.semaphore("dsem%d" % i)) for i in range(self.NDMA)]
        cnt = {s: 0 for s in streams}
        dcount = [0] * self.NDMA
        ndma = {"sp": 0, "pool": 0, "act": 0}
        half = self.NDMA // 2
        for op in ops:
            if op["dma"]:
                q = op["stream"]
                k = (ndma[q] % half) + (half if q == "pool" else 0)
                ndma[q] += 1
                op["prev"] = (k, dcount[k])
                dcount[k] += 16
                op["sig"] = (k, dcount[k])
            elif op["signal"]:
                cnt[op["stream"]] += 1
                op["sig"] = cnt[op["stream"]]
        final_dma = [(dsems[k], dcount[k]) for k in range(self.NDMA) if dcount[k] > 0]
        block = es.enter_context(nc.Block())

        def run_stream(sname, eng):
            waited = {}

            def wait(sem, key, val):
                if val <= 0 or waited.get(key, 0) >= val:
                    return
                eng.wait_ge(sem, val)
                waited[key] = val

            for i, op in enumerate(ops):
                if op["stream"] != sname:
                    continue
                for d in need[i]:
                    dop = ops[d]
                    if dop["dma"]:
                        k, v = dop["sig"]
                        wait(dsems[k], ("d", k), v)
                    else:
                        wait(sems[dop["stream"]], dop["stream"], dop["sig"])
                if op["dma"]:
                    k, pv = op["prev"]
                    wait(dsems[k], ("d", k), pv)
                ins = op["fn"](eng)
                if op["dma"]:
                    ins.then_inc(dsems[op["sig"][0]], 16)
                elif op["signal"]:
                    ins.then_inc(sems[sname], 1)
            if sname == "sp":
                for (sem, v) in final_dma:
                    eng.wait_ge(sem, v)

        @block.tensor
        def _(e):
            run_stream("pe", e)

        @block.scalar
        def _(e):
            run_stream("act", e)

        @block.vector
        def _(e):
            run_stream("dve", e)

        @block.gpsimd
        def _(e):
            run_stream("pool", e)

        @block.sync
        def _(e):
            run_stream("sp", e)

    def mm(self, out, lhsT, rhs, start=True, stop=True):
        return self.add("pe", lambda e: e.matmul(out, lhsT=lhsT, rhs=rhs, start=start, stop=stop),
                        [lhsT, rhs], [out])

    def tr(self, out, in_, ident):
        return self.add("pe", lambda e: e.transpose(out=out, in_=in_, identity=ident), [in_, ident], [out])

    def act(self, out, in_, func, bias=None, scale=None):
        kw = {}
        if bias is not None:
            kw["bias"] = bias
        if scale is not None:
            kw["scale"] = scale
        return self.add("act", lambda e: e.activation(out=out, in_=in_, func=func, **kw),
                        [in_, bias, scale], [out])

    def ts(self, eng, out, in0, s1, s2, op0, op1=None):
        if op1 is None:
            return self.add(eng, lambda e: e.tensor_scalar(out=out, in0=in0, scalar1=s1, scalar2=None, op0=op0),
                            [in0, s1], [out])
        return self.add(eng, lambda e: e.tensor_scalar(out=out, in0=in0, scalar1=s1, scalar2=s2, op0=op0, op1=op1),
                        [in0, s1, s2], [out])

    def tt(self, eng, out, in0, in1, op):
        return self.add(eng, lambda e: e.tensor_tensor(out=out, in0=in0, in1=in1, op=op), [in0, in1], [out])

    def stt(self, eng, out, in0, scalar, in1, op0, op1):
        return self.add(eng, lambda e: e.scalar_tensor_tensor(out=out, in0=in0, scalar=scalar, in1=in1,
                                                              op0=op0, op1=op1),
                        [in0, scalar, in1], [out])

    def copy(self, eng, out, in_):
        if eng == "act":
            return self.add("act", lambda e: e.copy(out=out, in_=in_), [in_], [out])
        return self.add(eng, lambda e: e.tensor_copy(out=out, in_=in_), [in_], [out])

    def memset(self, eng, out, val):
        return self.add(eng, lambda e: e.memset(out, val), [], [out])

    def dma(self, q, out, in_, slow=False):
        if slow:
            return self.add(q, lambda e: e.dma_start(out=out, in_=in_, allow_slow_non_contiguous=True),
                            [in_], [out], dma=True)
        return self.add(q, lambda e: e.dma_start(out=out, in_=in_), [in_], [out], dma=True)

    def gen(self, eng, fn, reads, writes):
        return self.add(eng, fn, reads, writes)


class TileDesc:
    pass


def build_program():
    nc = bass.Bass("TRN2", target_bir_lowering=False)

    def din(name, shape, dt=F32):
        return nc.dram_tensor(name, list(shape), dt, kind="ExternalInput").ap()

    def dout(name, shape, dt=F32):
        return nc.dram_tensor(name, list(shape), dt, kind="ExternalOutput").ap()

    def dint(name, shape, dt):
        return nc.dram_tensor(name, list(shape), dt, kind="Internal").ap()

    xp = din("xp", [NPS, SEQ, D])
    xs = din("xs", [NSS, SL, D])
    call = din("call", [NPS + NSS, D])
    spool = din("spool", [NSS, 15, PW])
    sC = din("sC", [NSS, NH, HD, HD])
    sn = din("sn", [NSS, NH, HD])
    sm = din("sm", [NSS, NH])
    w_ada = din("w_ada", [48, 128, 4096])
    b_ada = din("b_ada", [1, 6 * D])
    w_in = din("w_in", [52, 128, 4096])
    w_igfg_d = din("w_igfg", [128, 256])
    b_i = din("b_i", [1, NH])
    b_f = din("b_f", [1, NH])
    w_pool = din("w_pool", [4, 256, 256])
    pool_scale = din("pool_scale", [1, PW])
    gn_w = din("gn_w", [1, D])
    w_pa = din("w_pa", [8, 128, 2048])
    w_pb = din("w_pb", [8, 128, 4096])
    w_out = din("w_out", [8, 128, 4096])
    ln1_g = din("ln1_g", [1, D])
    ln1_b = din("ln1_b", [1, D])
    w_gu = din("w_gu", [DFF // 128, 128, 4096])
    w_down = din("w_down", [NFB * 8, 128, FBK * 256])
    ln2_g = din("ln2_g", [1, D])
    ln2_b = din("ln2_b", [1, D])

    yp = dout("yp", [NPS, SEQ, D])
    ys = dout("ys", [NSS, SL, D])
    o_pool_p = dout("pool_p", [NPS, 15, PW])
    o_C_p = dout("C_p", [NPS, NH, HD, HD])
    o_n_p = dout("n_p", [NPS, NH, HD])
    o_m_p = dout("m_p", [NPS, NH])
    o_pool_s = dout("pool_s", [NSS, 15, PW])
    o_C_s = dout("C_s", [NSS, NH, HD, HD])
    o_n_s = dout("n_s", [NSS, NH, HD])
    o_m_s = dout("m_s", [NSS, NH])

    wsc = {
        "in": dint("wsc_in", [52, 128, 4096], BF16),
        "pa": dint("wsc_pa", [8, 128, 2048], BF16),
        "pb": dint("wsc_pb", [8, 128, 4096], BF16),
        "out": dint("wsc_out", [8, 128, 4096], BF16),
        "gu": dint("wsc_gu", [DFF // 128, 128, 4096], BF16),
        "dn": dint("wsc_dn", [NFB * 8, 128, FBK * 256], BF16),
    }
    mod_d = dint("mod_d", [NPS + NSS, 6 * D], F32)

    es = ExitStack()
    with es:
        SB_BYTES = 207 * 1024
        S = es.enter_context(nc.sbuf_tensor("S", [128, SB_BYTES // 4], F32))
        Sb = S.bitcast(BF16)
        ps = [es.enter_context(nc.psum_tensor("ps%d" % i, [128, 512], F32)) for i in range(8)]
        psb = [p.bitcast(BF16) for p in ps]
        P = Prog(nc)

        cur = [0]

        def take(nbytes):
            o = cur[0]
            cur[0] += (nbytes + 63) // 64 * 64
            assert cur[0] <= SB_BYTES, ("SBUF overflow", cur[0])
            return o

        def V(off, shape, dt=F32):
            h = S if dt == F32 else Sb
            n = 1
            for s_ in shape[1:]:
                n *= s_
            b = off // _esz(dt)
            ap = h[0:shape[0], b:b + n]
            if len(shape) == 3:
                ap = ap.rearrange("p (a b) -> p a b", a=shape[1])
            elif len(shape) == 4:
                ap = ap.rearrange("p (a b c) -> p a b c", a=shape[1], b=shape[2])
            return ap

        def A(shape, dt=F32):
            n = _esz(dt)
            for s_ in shape[1:]:
                n *= s_
            return V(take(n), shape, dt)

        ident32 = A([128, 128])
        identb = A([128, 128], BF16)
        trineg = A([128, 128])
        onesneg = A([128, 128])
        ones32 = A([128, 128])
        sel128 = A([128, 128])
        sel32 = A([128, 128])
        caus = A([128, 128])
        maskT = A([128, 128])
        wpool = A([128, 4, 2, 256], BF16)
        wigfg = A([128, 16, 16], BF16)
        modT = A([128, NPS + NSS, 4, 16])
        psc = A([128, 8])
        gnw = A([128, 16])
        bif = A([128, 16])
        rc = A([128, 8, 16])
        Cst = A([128, NH, 2, 257])
        Cbf = A([128, 2, 2, 258], BF16)
        mst = A([128, NH])
        pcarry = A([128, 8, 16])
        bcs = [A([128, D]) for _ in range(2)]
        wslots = [take(SLOTB) for _ in range(NSLOT)]
        xres = A([128, NCH, D])
        phase0 = cur[0]
        uT = A([128, 16, TMAX], BF16)
        boutT = A([128, 16, TMAX], BF16)
        hT = boutT
        aoutT = A([128, 8, TMAX], BF16)
        PTB = 8 * NCH * (16 + LP) * 4
        r1sz = max(PTB + 16 * TMAX + 2 * 8 * (16 + LP) * 4, 32 * TMAX)
        r1 = take(r1sz)
        pT = V(r1, [128, 8, NCH, 16 + LP])
        yT = V(r1 + PTB, [128, 8, TMAX], BF16)
        poolA = V(r1 + PTB + 16 * TMAX, [128, 8, 16 + LP])
        poolB = V(r1 + PTB + 16 * TMAX + 8 * (16 + LP) * 4, [128, 8, 16 + LP])
        mergedT = V(r1, [128, 16, TMAX], BF16)
        T1off = r1
        T2off = r1 + NH * LP * 4
        save = cur[0]
        cur[0] = r1
        hb = []
        for _ in range(2):
            hb.append(dict(qT=A([128, 2, TMAX], BF16), ktok=A([128, NCH, 256], BF16), kT=A([128, 2, TMAX], BF16),
                           vaug=A([128, NCH, 258], BF16), sigo=A([128, NCH, 256], BF16),
                           kwb=A([128, NCH, 256], BF16)))
        assert cur[0] <= r1 + r1sz, (cur[0] - r1, r1sz)
        cur[0] = save
        dtx = take(max(NCH * NH * LP * 2, 2 * D * 2))
        DT = V(dtx, [128, NCH, NH, LP], BF16)
        xnb = [V(dtx + i * D * 2, [128, D], BF16) for i in range(2)]
        tmp0 = cur[0]
        tots = [A([128, 260]) for _ in range(2)]
        totA = [A([128, 260]) for _ in range(2)]
        hhs = [A([128, 256]) for _ in range(2)]
        hns = [A([128, 256], BF16) for _ in range(2)]
        kws = [A([128, 256], BF16) for _ in range(2)]
        sds = [A([128, LP], BF16) for _ in range(2)]
        sig_t = [A([128, 256]) for _ in range(2)]
        tmp_end = cur[0]
        cur[0] = tmp0
        sa_t = [A([128, TMAX]) for _ in range(2)]
        m1_t = [A([128, TMAX]) for _ in range(2)]
        tmp_end = max(tmp_end, cur[0])
        cur[0] = tmp0
        sil_t = [A([128, TMAX], BF16) for _ in range(2)]
        acc_t = [A([128, 256]) for _ in range(3)]
        tmp_end = max(tmp_end, cur[0])
        cur[0] = tmp0
        sptok = A([16, PW])
        pstok = A([16, PW])
        tmp_end = max(tmp_end, cur[0])
        cur[0] = tmp_end
        gv = {k: A([128, NCH, NH]) for k in ["ig", "g", "b", "mx", "mrow", "cneg", "winter", "expnm", "ws", "decay",
                                             "btot", "tdm", "tmp1", "tmp2", "tmp3"]}
        gx = A([128, 16])
        ge = A([128, NH])
        gl = A([128, NH])
        pmx = A([128, NH])
        gsl_t = [A([128, NCH, 256]) for _ in range(2)]
        sts = [dict(st=A([128, 4, 6]), mv=A([128, 2]), rstd=A([128, 1])) for _ in range(3)]
        gst = [dict(st=A([128, 6]), mv=A([128, 2]), rstd=A([128, 1]), den=A([128, 1]), rden=A([128, 1])) for _ in range(2)]
        small_end = cur[0]
        stg32 = [V(phase0 + i * 16384, [128, 4096]) for i in range(3)]
        stgb = [V(phase0 + 3 * 16384 + i * 8192, [128, 4096], BF16) for i in range(3)]
        assert phase0 + 3 * 16384 + 3 * 8192 <= SB_BYTES
        print("SBUF used", cur[0], "phase0", phase0)
        cur[0] = phase0 + 3 * 16384 + 3 * 8192
        cT = A([128, 16, 8])
        csil = A([128, 16, 8])
        brow = [A([1, 256]) for _ in range(2)]
        modsb = [A([8, 256]) for _ in range(2)]

        psi = [0]

        pspool = {"mode": "all", "c": 0, "p": 0}

        def PS(kind=None):
            if pspool["mode"] == "all":
                i = psi[0] % 8
                psi[0] += 1
                return i
            if kind == "proj":
                i = 5 + pspool["p"] % 3
                pspool["p"] += 1
                return i
            i = pspool["c"] % 5
            pspool["c"] += 1
            return i

        rr = {}

        def rot(key, n):
            v = rr.get(key, 0)
            rr[key] = v + 1
            return v % n

        def dbg(name, ap, dt=F32):
            if name not in DEBUG:
                return
            shp = list(ap.shape)
            dd = nc.dram_tensor("dbg_" + name, shp, dt, kind="ExternalOutput").ap()
            P.dma("pool", dd, ap)

        def aff(t, pattern, op, fill, base, cm):
            P.gen("pool", lambda e: e.affine_select(out=t, in_=t, pattern=pattern, compare_op=op, fill=fill,
                                                    base=base, channel_multiplier=cm), [t], [t])

        P.memset("pool", ident32, 1.0)
        aff(ident32, [[-1, 128]], ALU.is_equal, 0.0, 0, 1)
        P.copy("pool", identb, ident32)
        P.memset("pool", trineg, -1.0)
        aff(trineg, [[1, 128]], ALU.is_ge, 0.0, 0, -1)
        P.memset("pool", onesneg, -1.0)
        P.memset("pool", ones32, 1.0)
        P.memset("pool", sel128, 1.0)
        aff(sel128, [[0, 128]], ALU.is_equal, 0.0, -127, 1)
        P.memset("pool", sel32, 1.0)
        aff(sel32, [[0, 128]], ALU.is_equal, 0.0, -31, 1)
        P.memset("pool", caus, 0.0)
        aff(caus, [[-1, 128]], ALU.is_ge, -1e30, 0, 1)
        P.memset("pool", maskT, 0.0)
        aff(maskT, [[1, 128]], ALU.is_ge, -30000.0, 0, -1)
        for g_ in range(4):
            w_ = 2 << g_
            P.memset("pool", rc[:, 2 * g_:2 * g_ + 2, :], 1.0 / w_)
            for pos in range(w_ - 1):
                P.memset("pool", rc[:, 2 * g_:2 * g_ + 2, pos:pos + 1], 1.0 / (pos + 1))
        P.dma("sp", bif[:, 0:8], b_i.partition_broadcast(128).rearrange("p a b -> p (a b)"))
        P.dma("sp", bif[:, 8:16], b_f.partition_broadcast(128).rearrange("p a b -> p (a b)"))
        P.dma("sp", psc, pool_scale.rearrange("o (c p) -> p (o c)", p=128), slow=True)
        P.dma("sp", gnw, gn_w.rearrange("o (c p) -> p (o c)", p=128), slow=True)
        st0 = stg32[0]
        P.dma("sp", st0[:, 0:2048].rearrange("p (g c d) -> p g c d", g=4, c=2),
              w_pool.rearrange("g (c p) d -> p g c d", p=128))
        P.copy("dve", wpool, st0[:, 0:2048].rearrange("p (g c d) -> p g c d", g=4, c=2))
        st1 = stg32[1]
        P.dma("sp", st1[:, 0:256], w_igfg_d)
        P.copy("dve", wigfg, st1[:, 0:256].rearrange("p (k c) -> p k c", k=16))

        NSQ = NPS + NSS
        for s_ in range(NSQ):
            P.dma("sp", cT[:, :, s_], call[s_:s_ + 1, :].rearrange("o (k p) -> p (o k)", p=128), slow=True)
        P.act(csil[:, :, 0:NSQ], cT[:, :, 0:NSQ], AF.Silu)
        for u in range(48):
            sg = stg32[rot("stg", 3)]
            sgv = sg.rearrange("p (k c) -> p k c", k=16)
            P.dma("sp", sg, w_ada[u])
            br = brow[u % 2]
            P.dma("sp", br, b_ada[:, u * 256:(u + 1) * 256])
            pi = PS()
            for k in range(16):
                P.mm(ps[pi][:NSQ, 0:256], csil[:, k, 0:NSQ], sgv[:, k, :], start=(k == 0), stop=False)
            P.mm(ps[pi][:NSQ, 0:256], ones32[0:1, 0:NSQ], br[0:1, :], start=False, stop=True)
            mo = modsb[u % 2]
            P.copy("act", mo[:NSQ, :], ps[pi][:NSQ, 0:256])
            P.dma("pool", mod_d[:, u * 256:(u + 1) * 256], mo[:NSQ, :])
        for ki, koff in enumerate([0, D, 3 * D, 4 * D]):
            for s_ in range(NSQ):
                P.dma("sp", modT[:, s_, ki, :], mod_d[s_:s_ + 1, koff:koff + D].rearrange("o (k p) -> p (o k)", p=128),
                      slow=True)
        P.ts("dve", modT[:, :, 1, :], modT[:, :, 1, :], 1.0, None, ALU.add)
        P.ts("dve", modT[:, :, 3, :], modT[:, :, 3, :], 1.0, None, ALU.add)

        tiles = []
        for s_ in range(NPS):
            for ti in range(SEQ // TMAX):
                td = TileDesc()
                td.L = LP
                td.T = TMAX
                td.chunk_seq = [s_] * NCH
                td.groups = [(s_, list(range(NCH)))]
                td.x = [xp[s_, ti * TMAX + c * LP: ti * TMAX + (c + 1) * LP, :] for c in range(NCH)]
                td.y = [yp[s_, ti * TMAX + c * LP: ti * TMAX + (c + 1) * LP, :] for c in range(NCH)]
                td.start = [ti == 0 and c == 0 for c in range(NCH)]
                td.end = [ti == SEQ // TMAX - 1 and c == NCH - 1 for c in range(NCH)]
                td.sample = False
                td.first = (ti == 0)
                tiles.append(td)
        for ti in range(NSS // NCH):
            td = TileDesc()
            td.L = SL
            td.T = SL * NCH
            td.chunk_seq = [NPS + ti * NCH + c for c in range(NCH)]
            td.groups = [(NPS + ti * NCH + c, [c]) for c in range(NCH)]
            td.x = [xs[ti * NCH + c] for c in range(NCH)]
            td.y = [ys[ti * NCH + c] for c in range(NCH)]
            td.start = [False] * NCH
            td.end = [True] * NCH
            td.sample = True
            td.first = False
            tiles.append(td)

        if TILE_SEL is not None:
            tiles = [tiles[i] for i in TILE_SEL]

        def tile_units(td):
            u = [("in", i) for i in range(4)]
            for h in range(NH):
                u += [("in", 4 + h), ("in", 12 + h), ("in", 20 + h), ("in", 28 + h)]
            for jj in range(8):
                u += [("in", 36 + jj), ("pa", jj), ("in", 44 + jj), ("pb", jj)]
            u += [("out", cs) for cs in range(8)]
            for fb in range(NFB):
                u += [("gu", fb * FBK + j) for j in range(FBK)]
                u += [("dn", fb * 8 + cs) for cs in range(8)]
            return u

        order = []
        for td in tiles:
            order += tile_units(td)
        usz = {"in": 4096, "pa": 2048, "pb": 4096, "out": 4096, "gu": 4096, "dn": FBK * 256}
        wstate = dict(i=0, loaded=0)

        wfp = {"in": w_in, "pa": w_pa, "pb": w_pb, "out": w_out, "gu": w_gu, "dn": w_down}

        n_first = len(tile_units(tiles[0]))

        def wload(j):
            kind, idx = order[j]
            n = usz[kind]
            slot = wslots[j % NSLOT]
            if j < n_first:
                P.dma("pool", V(slot, [128, n], BF16), wfp[kind][idx])
                P.dma("sp", wsc[kind][idx], V(slot, [128, n], BF16))
            else:
                P.dma("sp", V(slot, [128, n], BF16), wsc[kind][idx])

        def wget(key):
            i = wstate["i"]
            assert order[i] == key, (order[i], key, i)
            while wstate["loaded"] < min(len(order), i + NSLOT):
                wload(wstate["loaded"])
                wstate["loaded"] += 1
            wstate["i"] = i + 1
            return wslots[i % NSLOT]

        def bcload(slot, row_ap):
            P.dma("pool", bcs[slot], row_ap.partition_broadcast(128).rearrange("p a b -> p (a b)"))

        def ln_stats(src, L):
            s_ = sts[rot("sts", 3)]
            for q in range(4):
                P.gen("dve", lambda e, q=q: e.bn_stats(out=s_["st"][:L, q, :], in_=src[:, q * 512:(q + 1) * 512]),
                      [src[:, q * 512:(q + 1) * 512]], [s_["st"][:L, q, :]])
            P.gen("dve", lambda e: e.bn_aggr(out=s_["mv"][:L, :], in_=s_["st"][:L, :, :]), [s_["st"][:L, :, :]],
                  [s_["mv"][:L, :]])
            P.act(s_["rstd"][:L, :], s_["mv"][:L, 1:2], AF.Ln, bias=EPS)
            P.act(s_["rstd"][:L, :], s_["rstd"][:L, :], AF.Exp, scale=-0.5)
            return s_["mv"][:L, 0:1], s_["rstd"][:L, :]

        def ln_to_T(src, L, c, seq, ksh, ksc):
            if LNCUT < 1:
                return
            mean, rstd = ln_stats(src, L)
            if LNCUT < 2:
                return
            xb = xnb[rot("xnb", 2)]
            P.ts("dve", xb[:L, :], src, mean, rstd, ALU.subtract, ALU.mult)
            if LNCUT < 3:
                return
            for g4 in range(4):
                pi = PS()
                for j in range(4):
                    fc = g4 * 4 + j
                    P.tr(psb[pi][:, j * L:(j + 1) * L], xb[:L, fc * 128:(fc + 1) * 128], identb[:L, :L])
                if LNCUT < 4:
                    continue
                for j in range(4):
                    fc = g4 * 4 + j
                    dst = uT[:, fc, c * L:(c + 1) * L]
                    src_ps = psb[pi][:, j * L:(j + 1) * L]
                    if (EVAC == 'act') or (EVAC == 'mix' and (fc % 2) == 0):
                        P.act(dst, src_ps, AF.Identity, bias=modT[:, seq, ksh, fc:fc + 1],
                              scale=modT[:, seq, ksc, fc:fc + 1])
                    else:
                        P.ts("dve", dst, src_ps, modT[:, seq, ksc, fc:fc + 1], modT[:, seq, ksh, fc:fc + 1],
                             ALU.mult, ALU.add)

        def ln_affine(dst, L, gslot, bslot):
            mean, rstd = ln_stats(dst, L)
            P.ts("dve", dst, dst, mean, rstd, ALU.subtract, ALU.mult)
            P.tt("dve", dst, dst, bcs[gslot][:L, :], ALU.mult)
            P.tt("dve", dst, dst, bcs[bslot][:L, :], ALU.add)

        for tix, td in enumerate(tiles):
            L = td.L
            T = td.T
            HPM = min(NH, 512 // L)
            sel = sel128 if L == 128 else sel32
            T1 = V(T1off, [128, NH, L])
            T2 = V(T2off, [128, NH, L])

            for c in range(NCH):
                P.dma("pool", xres[:L, c, :], td.x[c])
            for c in range(NCH):
                ln_to_T(xres[:L, c, :], L, c, td.chunk_seq[c], 0, 1)
            if tix == 0:
                dbg("uT", uT[:, :, 0:T], BF16)

            dbg("mst_beg%d" % tix, mst)
            if (td.sample or STOP_ALL) and STOP_AT == "S2":
                continue
            for c in range(NCH):
                seq = td.chunk_seq[c]
                if td.start[c]:
                    P.memset("pool", mst, 0.0)
                    P.memset("pool", Cst, 0.0)
                    P.memset("pool", pcarry, 0.0)
                if td.sample:
                    P.dma("pool", mst, sm[seq - NPS:seq - NPS + 1, :].partition_broadcast(128).rearrange("p a b -> p (a b)"))
                G = {k: v[:, c, :] for k, v in gv.items()}
                pi = PS()
                for k in range(16):
                    P.mm(ps[pi][:L, 0:16], uT[:, k, c * L:(c + 1) * L], wigfg[:, k, :], start=(k == 0), stop=(k == 15))
                P.tt("dve", gx[:L, :], ps[pi][:L, 0:16], bif[:L, :], ALU.add)
                P.copy("dve", G["ig"][:L, :], gx[:L, 0:8])
                P.ts("dve", gx[:L, 8:16], gx[:L, 8:16], -50.0, None, ALU.max)
                P.act(ge[:L, :], gx[:L, 8:16], AF.Exp, scale=-1.0)
                P.act(gl[:L, :], ge[:L, :], AF.Ln, bias=1.0)
                pi = PS()
                P.mm(ps[pi][:L, 0:8], trineg[:L, :L], gl[:L, :])
                P.mm(ps[pi][:128, 8:16], onesneg[:L, :128], gl[:L, :])
                P.copy("act", G["b"][:L, :], ps[pi][:L, 0:8])
                P.copy("act", G["btot"], ps[pi][:, 8:16])
                P.tt("dve", G["g"][:L, :], G["ig"][:L, :], G["b"][:L, :], ALU.subtract)
                P.tt("dve", T1[:L, :, :], ident32[:L, :L].unsqueeze(1).to_broadcast([L, NH, L]),
                     G["g"][:L, :].unsqueeze(2).to_broadcast([L, NH, L]), ALU.mult)
                for hq in range(NH // HPM):
                    pi = PS()
                    P.mm(ps[pi][:L, 0:HPM * L], ones32[:L, :L],
                         T1[:L, hq * HPM:(hq + 1) * HPM, :].rearrange("p a b -> p (a b)"))
                    P.tt("dve", T2[:L, hq * HPM:(hq + 1) * HPM, :],
                         ps[pi][:L, 0:HPM * L].rearrange("p (a b) -> p a b", a=HPM),
                         caus[:L, :L].unsqueeze(1).to_broadcast([L, HPM, L]), ALU.add)
                P.gen("dve", lambda e, L=L, T2=T2: e.tensor_reduce(out=pmx[:L, :], in_=T2[:L, :, :], axis=AX.X, op=ALU.max),
                      [T2[:L, :, :]], [pmx[:L, :]])
                P.tt("dve", G["mx"][:L, :], pmx[:L, :], mst[:L, :], ALU.max)
                P.tt("dve", G["mrow"][:L, :], G["b"][:L, :], G["mx"][:L, :], ALU.add)
                P.ts("dve", G["cneg"][:L, :], G["mx"][:L, :], -1.0, None, ALU.mult)
                P.tt("dve", G["tmp1"][:L, :], mst[:L, :], G["mx"][:L, :], ALU.subtract)
                P.act(G["winter"][:L, :], G["tmp1"][:L, :], AF.Exp)
                P.act(G["expnm"][:L, :], G["mrow"][:L, :], AF.Exp, scale=-1.0)
                P.tt("dve", T1[:L, :, :], ident32[:L, :L].unsqueeze(1).to_broadcast([L, NH, L]),
                     G["cneg"][:L, :].unsqueeze(2).to_broadcast([L, NH, L]), ALU.mult)
                for hq in range(NH // HPM):
                    pi = PS()
                    P.mm(ps[pi][:L, 0:HPM * L], ones32[:L, :L],
                         T1[:L, hq * HPM:(hq + 1) * HPM, :].rearrange("p a b -> p (a b)"))
                    P.tt("dve", T2[:L, hq * HPM:(hq + 1) * HPM, :],
                         ps[pi][:L, 0:HPM * L].rearrange("p (a b) -> p a b", a=HPM),
                         maskT[:L, :L].unsqueeze(1).to_broadcast([L, HPM, L]), ALU.add)
                for h in range(NH):
                    P.act(DT[:L, c, h, :L], T2[:L, h, :], AF.Exp, bias=G["g"][:L, h:h + 1])
                pi = PS()
                P.mm(ps[pi][:128, 0:8], sel[:L, :128], G["mrow"][:L, :])
                P.tt("dve", G["tdm"], G["btot"], ps[pi][:, 0:8], ALU.subtract)
                P.tt("dve", G["tmp2"][:L, :], G["g"][:L, :], G["tdm"][:L, :], ALU.add)
                P.act(G["ws"][:L, :], G["tmp2"][:L, :], AF.Exp)
                P.tt("dve", G["tmp3"], G["tdm"], mst, ALU.add)
                P.act(G["decay"], G["tmp3"], AF.Exp)
                P.copy("dve", mst, ps[pi][:, 0:8])
                if td.end[c]:
                    if td.sample:
                        P.dma("pool", o_m_s[seq - NPS:seq - NPS + 1, :], mst[0:1, :])
                    else:
                        P.dma("pool", o_m_p[seq:seq + 1, :], mst[0:1, :])
                if tix == 0 and c == 0:
                    dbg("DT", DT[:, 0, :, :], BF16)
                    dbg("mrow", gv["mrow"][:, 0, :])
                    dbg("gb", gv["b"][:, 0, :])

            dbg("mst_end%d" % tix, mst)
            dbg("mrowA%d" % tix, gv["mrow"][:, 0, :])
            dbg("mrowB%d" % tix, gv["mrow"][:, 1, :])
            dbg("mxB%d" % tix, gv["mx"][:, 1, :])
            dbg("bB%d" % tix, gv["b"][:, 1, :])
            if tix == 0:
                dbg("mrow1", gv["mrow"][:, 1, :])
                dbg("btot1", gv["btot"][:, 1, :])
                dbg("mx1", gv["mx"][:, 1, :])

            if (td.sample or STOP_ALL) and STOP_AT == "S3":
                continue
            for i in range(4):
                wo = wget(("in", i))
                wv = V(wo, [128, 16, 256], BF16)
                for half in range(2):
                    cc = 2 * i + half
                    pi = PS()
                    for k in range(16):
                        P.mm(ps[pi][:, 0:T], wv[:, k, half * 128:(half + 1) * 128], uT[:, k, 0:T],
                             start=(k == 0), stop=(k == 15))
                    P.copy("act", pT[:, cc, :, 16:16 + L], ps[pi][:, 0:T].rearrange("p (a b) -> p a b", a=NCH))
            for c in range(NCH):
                seq = td.chunk_seq[c]
                W_ = 16 + L
                if td.sample:
                    P.dma("pool", sptok[0:15, :], spool[seq - NPS])
                    pi = PS()
                    for cc in range(8):
                        P.tr(ps[pi][:, cc * 16 + 1:cc * 16 + 16], sptok[0:15, cc * 128:(cc + 1) * 128],
                             ident32[0:15, 0:15])
                    P.copy("dve", pT[:, :, c, 1:16],
                           ps[pi][:, 0:128].rearrange("p (a b) -> p a b", a=8)[:, :, 1:16])
                elif c == 0:
                    P.copy("pool", pT[:, :, 0, 1:16], pcarry[:, :, 1:16])
                else:
                    P.copy("pool", pT[:, :, c, 1:16], pT[:, :, c - 1, L + 1:L + 16])
                Pc = pT[:, :, c, :]
                P.tt("dve", poolA[:, :, 2:W_], Pc[:, :, 2:W_], Pc[:, :, 1:W_ - 1], ALU.add)
                P.tt("dve", poolB[:, 2:8, 4:W_], poolA[:, 2:8, 4:W_], poolA[:, 2:8, 2:W_ - 2], ALU.add)
                P.tt("dve", poolA[:, 4:8, 8:W_], poolB[:, 4:8, 8:W_], poolB[:, 4:8, 4:W_ - 4], ALU.add)
                P.tt("dve", poolB[:, 6:8, 16:W_], poolA[:, 6:8, 16:W_], poolA[:, 6:8, 8:W_ - 8], ALU.add)
                for g_ in range(4):
                    src = (poolA if g_ % 2 == 0 else poolB)
                    P.stt("dve", yT[:, 2 * g_:2 * g_ + 2, c * L:(c + 1) * L], src[:, 2 * g_:2 * g_ + 2, 16:W_],
                          1.0 / (2 << g_), Pc[:, 2 * g_:2 * g_ + 2, 16:W_], ALU.mult, ALU.subtract)
                    if td.start[c]:
                        P.tt("dve", src[:, 2 * g_:2 * g_ + 2, 16:32], src[:, 2 * g_:2 * g_ + 2, 16:32],
                             rc[:, 2 * g_:2 * g_ + 2, :], ALU.mult)
                        P.tt("dve", yT[:, 2 * g_:2 * g_ + 2, c * L:c * L + 16], src[:, 2 * g_:2 * g_ + 2, 16:32],
                             Pc[:, 2 * g_:2 * g_ + 2, 16:32], ALU.subtract)
                if td.end[c]:
                    for hq in range(2):
                        pi = PS()
                        for q in range(4):
                            cc = hq * 4 + q
                            P.tr(ps[pi][0:15, q * 128:(q + 1) * 128], pT[:, cc, c, L + 1:L + 16], ident32[:, :])
                        P.copy("act", pstok[0:15, hq * 512:(hq + 1) * 512], ps[pi][0:15, 0:512])
                    if td.sample:
                        P.dma("pool", o_pool_s[seq - NPS], pstok[0:15, :])
                    else:
                        P.dma("pool", o_pool_p[seq], pstok[0:15, :])
            if not td.sample:
                P.copy("pool", pcarry[:, :, 1:16], pT[:, :, NCH - 1, L + 1:L + 16])
            if tix == 0:
                dbg("yT", yT[:, :, 0:T], BF16)
            for g_ in range(4):
                for dc in range(2):
                    pi = PS()
                    for k in range(2):
                        P.mm(ps[pi][:, 0:T], wpool[:, g_, k, dc * 128:(dc + 1) * 128], yT[:, 2 * g_ + k, 0:T],
                             start=(k == 0), stop=(k == 1))
                    P.ts("dve", aoutT[:, 2 * g_ + dc, 0:T], ps[pi][:, 0:T], psc[:, 2 * g_ + dc:2 * g_ + dc + 1], None,
                         ALU.mult)
            if tix == 0:
                dbg("aoutT", aoutT[:, :, 0:T], BF16)

            if (td.sample or STOP_ALL) and STOP_AT == "S4":
                continue
            if len(td.groups) == 1:
                bcload(0, mod_d[td.groups[0][0]:td.groups[0][0] + 1, 2 * D:3 * D])
            def proj_jobs(h):
                B_ = hb[h % 2]
                qT, ktok, kT, vaug, sigo, kwb = B_["qT"], B_["ktok"], B_["kT"], B_["vaug"], B_["sigo"], B_["kwb"]
                jobs = []
                st = {}

                def unit(key, shape):
                    if key not in st:
                        st[key] = V(wget(key), shape, BF16)
                    return st[key]

                for dc in range(2):
                    def mmq(dc=dc):
                        wv = unit(("in", 4 + h), [128, 16, 256])
                        pi = PS("proj")
                        for k in range(16):
                            P.mm(ps[pi][:, 0:T], wv[:, k, dc * 128:(dc + 1) * 128], uT[:, k, 0:T],
                                 start=(k == 0), stop=(k == 15))
                        return pi

                    def evq(pi, dc=dc):
                        P.copy("dve", qT[:, dc, 0:T], ps[pi][:, 0:T])
                    jobs.append((mmq, evq))
                for c in range(NCH):
                    def mmk(c=c):
                        wv = unit(("in", 12 + h), [128, 16, 256])
                        pi = PS("proj")
                        for k in range(16):
                            P.mm(ps[pi][:L, 0:256], uT[:, k, c * L:(c + 1) * L], wv[:, k, :], start=(k == 0), stop=(k == 15))
                        return pi

                    def evk(pi, c=c):
                        P.act(ktok[:L, c, :], ps[pi][:L, 0:256], AF.Identity, scale=1.0 / 16.0)
                    jobs.append((mmk, evk))
                for c in range(NCH):
                    def mmt(c=c):
                        pj = PS("proj")
                        for dc in range(2):
                            P.tr(psb[pj][:, dc * L:(dc + 1) * L], ktok[:L, c, dc * 128:(dc + 1) * 128], identb[:L, :L])
                        return pj

                    def evt(pj, c=c):
                        P.copy("dve", kT[:, :, c * L:(c + 1) * L],
                               psb[pj][:, 0:2 * L].rearrange("p (a b) -> p a b", a=2))
                        P.ts("dve", kwb[:L, c, :], ktok[:L, c, :], gv["ws"][:L, c, h:h + 1], None, ALU.mult)
                    jobs.append((mmt, evt))
                for c in range(NCH):
                    def mmv(c=c):
                        wv = unit(("in", 20 + h), [128, 16, 256])
                        pi = PS("proj")
                        for k in range(16):
                            P.mm(ps[pi][:L, 0:256], uT[:, k, c * L:(c + 1) * L], wv[:, k, :], start=(k == 0), stop=(k == 15))
                        return pi

                    def evv(pi, c=c):
                        P.copy("dve", vaug[:L, c, 0:256], ps[pi][:L, 0:256])
                        P.memset("pool", vaug[:L, c, 256:257], 1.0)
                    jobs.append((mmv, evv))
                for c in range(NCH):
                    def mmo(c=c):
                        wv = unit(("in", 28 + h), [128, 16, 256])
                        pi = PS("proj")
                        for k in range(16):
                            P.mm(ps[pi][:L, 0:256], uT[:, k, c * L:(c + 1) * L], wv[:, k, :], start=(k == 0), stop=(k == 15))
                        return pi

                    def evo(pi, c=c):
                        se = sig_t[rot("sig", 2)]
                        P.act(se[:L, :], ps[pi][:L, 0:256], AF.Exp, scale=-1.0)
                        P.act(se[:L, :], se[:L, :], AF.Ln, bias=1.0)
                        P.act(sigo[:L, c, :], se[:L, :], AF.Exp, scale=-1.0)
                    jobs.append((mmo, evo))
                return jobs

            pspool["mode"] = "split"
            SPF = 4

            def sslot(h_, c_):
                return (h_ * NCH + c_) % NH if td.sample else h_

            def sload(step):
                h_, c_ = step // NCH, step % NCH
                if h_ >= NH:
                    return
                sq = td.chunk_seq[c_] - NPS
                sl_ = sslot(h_, c_)
                P.dma("pool", Cst[:, sl_, :, 0:256], sC[sq, h_].rearrange("(c p) e -> p c e", p=128))
                P.dma("pool", Cst[:, sl_, :, 256], sn[sq, h_:h_ + 1, :].rearrange("o (c p) -> p (o c)", p=128),
                      slow=True)

            if td.sample:
                for st_ in range(SPF):
                    sload(st_)
            for (mmf, evf) in proj_jobs(0):
                evf(mmf())
            for h in range(NH):
                B_ = hb[h % 2]
                qT, ktok, kT, vaug, sigo, kwb = B_["qT"], B_["ktok"], B_["kT"], B_["vaug"], B_["sigo"], B_["kwb"]
                nxt = list(proj_jobs(h + 1)) if h + 1 < NH else []
                per_chunk = -(-len(nxt) // NCH)

                if not td.sample:
                    P.copy("act", Cbf[:, h % 2, :, 0:257], Cst[:, h, :, :])
                for c in range(NCH):
                    seq = td.chunk_seq[c]
                    G = {k: v[:, c, :] for k, v in gv.items()}
                    hs = sslot(h, c)
                    if td.sample:
                        sload(h * NCH + c + SPF)
                        P.copy("act", Cbf[:, h % 2, :, 0:257], Cst[:, hs, :, :])
                    pS = PS()
                    for dc in range(2):
                        P.mm(ps[pS][:L, 0:L], kT[:, dc, c * L:(c + 1) * L], qT[:, dc, c * L:(c + 1) * L],
                             start=(dc == 0), stop=(dc == 1))
                    sd = sds[rot("sd", 2)]
                    P.tt("dve", sd[:L, :L], ps[pS][:L, 0:L], DT[:L, c, h, :L], ALU.mult)
                    pA = PS()
                    P.mm(ps[pA][:L, 0:257], sd[:L, :L], vaug[:L, c, 0:257])
                    pB = PS()
                    for dc in range(2):
                        P.mm(ps[pB][:L, 0:257], qT[:, dc, c * L:(c + 1) * L], Cbf[:, h % 2, dc, 0:257],
                             start=(dc == 0), stop=(dc == 1))
                    pUs = []
                    for dc in range(2):
                        pU = PS()
                        P.mm(ps[pU][:, 0:257], kwb[:L, c, dc * 128:(dc + 1) * 128], vaug[:L, c, 0:257])
                        pUs.append(pU)
                    todo = nxt[:per_chunk]
                    nxt = nxt[per_chunk:]
                    deferred = [(evf, mmf()) for (mmf, evf) in todo[:3]]
                    for dc in range(2):
                        P.stt("dve", Cst[:, hs, dc, :], Cst[:, hs, dc, :], G["decay"][:, h:h + 1], ps[pUs[dc]][:, 0:257],
                              ALU.mult, ALU.add)
                    P.copy("act", Cbf[:, h % 2, :, 0:257], Cst[:, hs, :, :])
                    tA = totA[rot("totA", 2)]
                    tot = tots[rot("tot", 2)]
                    P.copy("act", tA[:L, 0:257], ps[pA][:L, 0:257])
                    P.stt("dve", tot[:L, 0:257], ps[pB][:L, 0:257], G["winter"][:L, h:h + 1], tA[:L, 0:257],
                          ALU.mult, ALU.add)
                    gs = gst[rot("gst", 2)]
                    P.stt("dve", gs["den"][:L, :], tot[:L, 256:257], -1.0, tot[:L, 256:257], ALU.mult, ALU.max)
                    P.tt("dve", gs["den"][:L, :], gs["den"][:L, :], G["expnm"][:L, h:h + 1], ALU.max)
                    P.gen("dve", lambda e, gs=gs, L=L: e.reciprocal(out=gs["rden"][:L, :], in_=gs["den"][:L, :]),
                          [gs["den"][:L, :]], [gs["rden"][:L, :]])
                    hh = hhs[rot("hh", 2)]
                    P.stt("dve", hh[:L, :], tot[:L, 0:256], gs["rden"][:L, :], sigo[:L, c, :], ALU.mult, ALU.mult)
                    P.gen("dve", lambda e, gs=gs, hh=hh, L=L: e.bn_stats(out=gs["st"][:L, :], in_=hh[:L, :]),
                          [hh[:L, :]], [gs["st"][:L, :]])
                    P.gen("dve", lambda e, gs=gs, L=L: e.bn_aggr(out=gs["mv"][:L, :], in_=gs["st"][:L, :]),
                          [gs["st"][:L, :]], [gs["mv"][:L, :]])
                    P.act(gs["rstd"][:L, :], gs["mv"][:L, 1:2], AF.Ln, bias=EPS)
                    P.act(gs["rstd"][:L, :], gs["rstd"][:L, :], AF.Exp, scale=-0.5)
                    hn = hns[rot("hn", 2)]
                    P.ts("dve", hn[:L, :], hh[:L, :], gs["mv"][:L, 0:1], gs["rstd"][:L, :], ALU.subtract, ALU.mult)
                    pj = PS()
                    for dc in range(2):
                        P.tr(psb[pj][:, dc * L:(dc + 1) * L], hn[:L, dc * 128:(dc + 1) * 128], identb[:L, :L])
                    for dc in range(2):
                        P.act(boutT[:, 2 * h + dc, c * L:(c + 1) * L], psb[pj][:, dc * L:(dc + 1) * L], AF.Identity,
                              scale=gnw[:, 2 * h + dc:2 * h + dc + 1])
                    if td.end[c]:
                        oC = o_C_s[seq - NPS, h] if td.sample else o_C_p[seq, h]
                        on = o_n_s[seq - NPS, h:h + 1, :] if td.sample else o_n_p[seq, h:h + 1, :]
                        P.dma("pool", oC.rearrange("(c p) e -> p c e", p=128), Cst[:, hs, :, 0:256])
                        P.dma("pool", on.rearrange("o (c p) -> p (o c)", p=128), Cst[:, hs, :, 256], slow=True)
                    for (evf, pi_) in deferred:
                        evf(pi_)
                    for (mmf, evf) in todo[3:]:
                        evf(mmf())
                    if tix == 0 and c == 0 and h == 0:
                        dbg("hh", hh[:, :])
                        dbg("tot", tot[:, 0:257])
                for (mmf, evf) in nxt:
                    evf(mmf())
            pspool["mode"] = "all"
            if tix == 0:
                dbg("boutT", boutT[:, :, 0:T], BF16)

            if (td.sample or STOP_ALL) and STOP_AT == "S5":
                continue
            bcload(1, ln1_g)
            for jj in range(8):
                wg = V(wget(("in", 36 + jj)), [128, 16, 256], BF16)
                sas = []
                for half in range(2):
                    pi = PS()
                    for k in range(16):
                        P.mm(ps[pi][:, 0:T], wg[:, k, half * 128:(half + 1) * 128], uT[:, k, 0:T],
                             start=(k == 0), stop=(k == 15))
                    sa = sa_t[half]
                    P.act(sa[:, 0:T], ps[pi][:, 0:T], AF.Sigmoid)
                    sas.append(sa)
                wa = V(wget(("pa", jj)), [128, 8, 256], BF16)
                m1s = []
                for half in range(2):
                    pi = PS()
                    for k in range(8):
                        P.mm(ps[pi][:, 0:T], wa[:, k, half * 128:(half + 1) * 128], aoutT[:, k, 0:T],
                             start=(k == 0), stop=(k == 7))
                    m1 = m1_t[half]
                    P.tt("dve", m1[:, 0:T], sas[half][:, 0:T], ps[pi][:, 0:T], ALU.mult)
                    m1s.append(m1)
                wg = V(wget(("in", 44 + jj)), [128, 16, 256], BF16)
                sbs = []
                for half in range(2):
                    pi = PS()
                    for k in range(16):
                        P.mm(ps[pi][:, 0:T], wg[:, k, half * 128:(half + 1) * 128], uT[:, k, 0:T],
                             start=(k == 0), stop=(k == 15))
                    sb_ = sa_t[half]
                    P.act(sb_[:, 0:T], ps[pi][:, 0:T], AF.Sigmoid)
                    sbs.append(sb_)
                wb = V(wget(("pb", jj)), [128, 16, 256], BF16)
                for half in range(2):
                    pi = PS()
                    for k in range(16):
                        P.mm(ps[pi][:, 0:T], wb[:, k, half * 128:(half + 1) * 128], boutT[:, k, 0:T],
                             start=(k == 0), stop=(k == 15))
                    P.tt("dve", sbs[half][:, 0:T], sbs[half][:, 0:T], ps[pi][:, 0:T], ALU.mult)
                    P.tt("dve", mergedT[:, 2 * jj + half, 0:T], m1s[half][:, 0:T], sbs[half][:, 0:T], ALU.add)
            if tix == 0:
                dbg("mergedT", mergedT[:, :, 0:T], BF16)

            if (td.sample or STOP_ALL) and STOP_AT == "S6":
                continue
            multi = len(td.groups) > 1
            s0 = td.chunk_seq[0]

            def gate_rows(koff, cs, slot):
                if not multi:
                    return [bcs[slot][:L, cs * 256:(cs + 1) * 256]] * NCH
                gsl = gsl_t[rot("gsl", 2)]
                P.dma("pool", gsl[:L, :, :],
                      mod_d[s0:s0 + NCH, koff + cs * 256:koff + (cs + 1) * 256].partition_broadcast(L))
                return [gsl[:L, c, :] for c in range(NCH)]

            if True:
                for cs in range(8):
                    wv = V(wget(("out", cs)), [128, 16, 256], BF16)
                    grow = gate_rows(2 * D, cs, 0)
                    for c in range(NCH):
                        pi = PS()
                        for k in range(16):
                            P.mm(ps[pi][:L, 0:256], mergedT[:, k, c * L:(c + 1) * L], wv[:, k, :],
                                 start=(k == 0), stop=(k == 15))
                        ac = acc_t[rot("acc", 3)]
                        P.tt("dve", ac[:L, :], ps[pi][:L, 0:256], grow[c], ALU.mult)
                        xs_ = xres[:L, c, cs * 256:(cs + 1) * 256]
                        P.stt("dve", xs_, xs_, ALPHA, ac[:L, :], ALU.mult, ALU.add)
            bcload(0, ln1_b)
            for c in range(NCH):
                ln_affine(xres[:L, c, :], L, 1, 0)
            if tix == 0:
                dbg("x1", xres[:, 0, :])
            if len(td.groups) == 1:
                bcload(1, mod_d[td.groups[0][0]:td.groups[0][0] + 1, 5 * D:6 * D])
            for c in range(NCH):
                ln_to_T(xres[:L, c, :], L, c, td.chunk_seq[c], 2, 3)
            bcload(0, ln2_g)

            if (td.sample or STOP_ALL) and STOP_AT == "S8":
                continue
            for fb in range(NFB):
                for j in range(FBK):
                    wv = V(wget(("gu", fb * FBK + j)), [128, 2, 16, 128], BF16)
                    pg = PS()
                    for k in range(16):
                        P.mm(ps[pg][:, 0:T], wv[:, 0, k, :], uT[:, k, 0:T], start=(k == 0), stop=(k == 15))
                    pu = PS()
                    for k in range(16):
                        P.mm(ps[pu][:, 0:T], wv[:, 1, k, :], uT[:, k, 0:T], start=(k == 0), stop=(k == 15))
                    sl = sil_t[rot("sil", 2)]
                    P.act(sl[:, 0:T], ps[pg][:, 0:T], AF.Silu)
                    P.tt("dve", hT[:, j, 0:T], sl[:, 0:T], ps[pu][:, 0:T], ALU.mult)
                if True:
                    for cs in range(8):
                        wv = V(wget(("dn", fb * 8 + cs)), [128, FBK, 256], BF16)
                        grow = gate_rows(5 * D, cs, 1)
                        for c in range(NCH):
                            pi = PS()
                            for k in range(FBK):
                                P.mm(ps[pi][:L, 0:256], hT[:, k, c * L:(c + 1) * L], wv[:, k, :],
                                     start=(k == 0), stop=(k == FBK - 1))
                            ac = acc_t[rot("acc", 3)]
                            P.tt("dve", ac[:L, :], ps[pi][:L, 0:256], grow[c], ALU.mult)
                            xs_ = xres[:L, c, cs * 256:(cs + 1) * 256]
                            if fb == 0:
                                P.stt("dve", xs_, xs_, ALPHA, ac[:L, :], ALU.mult, ALU.add)
                            else:
                                P.tt("dve", xs_, xs_, ac[:L, :], ALU.add)
            bcload(1, ln2_b)
            for c in range(NCH):
                ln_affine(xres[:L, c, :], L, 0, 1)
                P.dma("pool", td.y[c], xres[:L, c, :])

        assert STOP_AT is not None or wstate["i"] == len(order)
        P.emit(es)
    return nc


def kernel(x_prompt, x_sample, c_prompt, c_sample, state_pool, state_mlstm_C, state_mlstm_n, state_mlstm_m,
           w_ada, b_ada, w_in, b_i, b_f, w_pool, pool_scale, gn_w, w_pa, w_pb, w_out, ln1_g, ln1_b,
           w_gate, w_up, w_down, ln2_g, ln2_b):
    n = 8
    f = lambda a: np.ascontiguousarray(np.asarray(a, dtype=np.float32))
    def blk(w, c0, nc_, r0=0, nr=None):
        w = np.asarray(w)
        nr = w.shape[0] - r0 if nr is None else nr
        return w[r0:r0 + nr, c0:c0 + nc_].reshape(nr // 128, 128, nc_).transpose(1, 0, 2)

    wi = np.asarray(w_in[0], dtype=np.float32)
    in_cols = [256 * i for i in range(36)] + [COL_GA + 256 * i for i in range(8)] + [COL_GB + 256 * i for i in range(8)]
    wg_, wu_, wd_ = (np.asarray(a[0], dtype=np.float32) for a in (w_gate, w_up, w_down))
    shared = {
        "w_ada": f(np.stack([blk(w_ada[0], 256 * u, 256).reshape(128, 4096) for u in range(48)])),
        "w_in": f(np.stack([blk(wi, c0, 256).reshape(128, 4096) for c0 in in_cols])),
        "w_igfg": f(blk(wi, COL_IG, 16).reshape(128, 256)),
        "w_pa": f(np.stack([blk(w_pa[0], 256 * i, 256).reshape(128, 2048) for i in range(8)])),
        "w_pb": f(np.stack([blk(w_pb[0], 256 * i, 256).reshape(128, 4096) for i in range(8)])),
        "w_out": f(np.stack([blk(w_out[0], 256 * i, 256).reshape(128, 4096) for i in range(8)])),
        "w_gu": f(np.stack([np.concatenate([blk(wg_, 128 * j, 128).reshape(128, 2048),
                                            blk(wu_, 128 * j, 128).reshape(128, 2048)], axis=1)
                            for j in range(DFF // 128)])),
        "w_down": f(np.stack([blk(wd_, 256 * cs, 256, fb * FBK * 128, FBK * 128).reshape(128, FBK * 256)
                              for fb in range(NFB) for cs in range(8)])),
        "b_ada": f(b_ada), "b_i": f(b_i), "b_f": f(b_f), "w_pool": f(w_pool[0]), "pool_scale": f(pool_scale),
        "gn_w": f(gn_w), "ln1_g": f(ln1_g), "ln1_b": f(ln1_b), "ln2_g": f(ln2_g), "ln2_b": f(ln2_b),
    }
    x_prompt = np.asarray(x_prompt, dtype=np.float32)
    x_sample = np.asarray(x_sample, dtype=np.float32)
    in_maps = []
    for i in range(n):
        m = dict(shared)
        m["xp"] = f(x_prompt[NPS * i:NPS * (i + 1)])
        m["xs"] = f(x_sample[NSS * i:NSS * (i + 1)])
        m["call"] = f(np.concatenate([np.asarray(c_prompt)[NPS * i:NPS * (i + 1)],
                                      np.asarray(c_sample)[NSS * i:NSS * (i + 1)]], axis=0))
        m["spool"] = f(np.asarray(state_pool)[0, NSS * i:NSS * (i + 1)])
        m["sC"] = f(np.asarray(state_mlstm_C)[0, NSS * i:NSS * (i + 1)])
        m["sn"] = f(np.asarray(state_mlstm_n)[0, NSS * i:NSS * (i + 1)])
        m["sm"] = f(np.asarray(state_mlstm_m)[0, NSS * i:NSS * (i + 1)])
        in_maps.append(m)
    nc = build_program()
    res = run_bass_kernel_spmd(nc, in_maps, core_ids=list(range(n)))
    R = res.results
    cat = lambda k: np.concatenate([np.asarray(r[k], dtype=np.float32) for r in R], axis=0)
    if DEBUG:
        kernel.debug = {k: np.asarray(R[0]["dbg_" + k]) for k in DEBUG if ("dbg_" + k) in R[0]}
    return (cat("yp"), cat("ys"), cat("pool_p")[None], cat("C_p")[None], cat("n_p")[None], cat("m_p")[None],
            cat("pool_s")[None], cat("C_s")[None], cat("n_s")[None], cat("m_s")[None])
```

```python
import numpy as np
from contextlib import ExitStack
import concourse.bass as bass
import concourse.mybir as mybir
from concourse.bass_utils import run_bass_kernel_spmd

F32 = mybir.dt.float32
BF16 = mybir.dt.bfloat16
AF = mybir.ActivationFunctionType
ALU = mybir.AluOpType
AX = mybir.AxisListType

D = 2048
NIN = 13328
DFF = 5632
NH = 8
HD = 256
PW = 1024
SEQ = 2048
NPS = 2
NSS = 4
SL = 32
ALPHA = float(2.0 ** 0.25)
EPS = 1e-5
NCH = 4
LP = 128
TMAX = NCH * LP
NFB = 4
FBK = 11
NSLOT = 3
SLOTB = 8192
COL_IG = 9216
COL_GA = 9232
COL_GB = 11280
DEBUG = {}
TILE_SEL = None
STOP_AT = None
STOP_ALL = False
LNCUT = 9
EVAC = 'mix'


def _esz(dt):
    return 4 if dt == F32 else 2


class Prog:
    NDMA = 24

    def __init__(self, nc):
        self.nc = nc
        self.ops = []
        self.W = {}
        self.R = {}

    def region(self, ap):
        t = ap.tensor
        name = t.name
        esz = _esz(ap.dtype)
        dims = list(ap.ap)
        off = ap.offset
        kind = type(t).__name__
        if "PSum" in kind:
            return (name, 0, 128, 0, 1 << 40)
        if "SB" in kind:
            rowb = _esz(t.dtype)
            for s in list(t.shape)[1:]:
                rowb *= s
            rowlen = rowb // esz
            p0 = off // rowlen
            col = off % rowlen
            pstep, pcnt = dims[0]
            npart = pcnt if pstep != 0 else 1
            ext = 1
            for st, cn in dims[1:]:
                ext += (cn - 1) * abs(st)
            return (name, p0, p0 + npart, col * esz, (col + ext) * esz)
        ext = 1
        for st, cn in dims:
            ext += (cn - 1) * abs(st)
        return (name, 0, 1, off * esz, (off + ext) * esz)

    @staticmethod
    def _ov(a, b):
        return a[1] < b[2] and b[1] < a[2] and a[3] < b[4] and b[3] < a[4]

    @staticmethod
    def _cov(a, b):
        return b[1] <= a[1] and a[2] <= b[2] and b[3] <= a[3] and a[4] <= b[4]

    def _buckets(self, r):
        name = r[0]
        if name.startswith("ps"):
            return [(name, 0)]
        bs = 1024 if name == "S" else (1 << 18)
        return [(name, b) for b in range(r[3] // bs, (r[4] - 1) // bs + 1)]

    def add(self, stream, fn, reads, writes, dma=False):
        idx = len(self.ops)
        deps = {}
        rr = [self.region(a) for a in reads if a is not None and not isinstance(a, (int, float))]
        ww = [self.region(a) for a in writes]
        ww += [r for r in rr if r[0].startswith("ps")]
        for r in rr:
            for bk in self._buckets(r):
                for (reg, o) in self.W.get(bk, ()):
                    if self._ov(r, reg):
                        deps[o] = True
        for w in ww:
            for bk in self._buckets(w):
                for (reg, o) in self.W.get(bk, ()):
                    if self._ov(w, reg):
                        deps.setdefault(o, False)
                for (reg, o) in self.R.get(bk, ()):
                    if self._ov(w, reg):
                        deps.setdefault(o, False)
        for w in ww:
            for bk in self._buckets(w):
                self.W[bk] = [(reg, o) for (reg, o) in self.W.get(bk, ()) if not self._cov(reg, w)] + [(w, idx)]
                self.R[bk] = [(reg, o) for (reg, o) in self.R.get(bk, ()) if not self._cov(reg, w)]
        for r in rr:
            for bk in self._buckets(r):
                lst = self.R.setdefault(bk, [])
                keep = []
                for (reg, o) in lst:
                    if o == idx:
                        if reg != r:
                            keep.append((reg, o))
                        continue
                    oo = self.ops[o]
                    if reg == r and oo["stream"] == stream and (not oo["dma"]) and (not dma):
                        continue
                    keep.append((reg, o))
                keep.append((r, idx))
                lst[:] = keep
        deps.pop(idx, None)
        self.ops.append(dict(stream=stream, fn=fn, deps=deps, dma=dma, signal=False))
        return idx

    def emit(self, es):
        nc = self.nc
        ops = self.ops
        streams = ["pe", "act", "dve", "pool", "sp"]
        need = []
        for i, op in enumerate(ops):
            nl = []
            for d, raw in op["deps"].items():
                dop = ops[d]
                if dop["dma"] or op["dma"]:
                    nl.append(d)
                elif dop["stream"] == op["stream"]:
                    if op["stream"] != "pe":
                        nl.append(d)
                else:
                    nl.append(d)
            for d in nl:
                ops[d]["signal"] = True
            need.append(nl)
        sems = {s: es.enter_context(nc.semaphore("sem_" + s)) for s in streams}
        dsems = [es.enter_context(nc.semaphore("dsem%d" % i)) for i in range(self.NDMA)]
        cnt = {s: 0 for s in streams}
        dcount = [0] * self.NDMA
        ndma = {"sp": 0, "pool": 0, "act": 0}
        half = self.NDMA // 2
        for op in ops:
            if op["dma"]:
                q = op["stream"]
                k = (ndma[q] % half) + (half if q == "pool" else 0)
                ndma[q] += 1
                op["prev"] = (k, dcount[k])
                dcount[k] += 16
                op["sig"] = (k, dcount[k])
            elif op["signal"]:
                cnt[op["stream"]] += 1
                op["sig"] = cnt[op["stream"]]
        final_dma = [(dsems[k], dcount[k]) for k in range(self.NDMA) if dcount[k] > 0]
        block = es.enter_context(nc.Block())

        def run_stream(sname, eng):
            waited = {}

            def wait(sem, key, val):
                if val <= 0 or waited.get(key, 0) >= val:
                    return
                eng.wait_ge(sem, val)
                waited[key] = val

            for i, op in enumerate(ops):
                if op["stream"] != sname:
                    continue
                for d in need[i]:
                    dop = ops[d]
                    if dop["dma"]:
                        k, v = dop["sig"]
                        wait(dsems[k], ("d", k), v)
                    else:
                        wait(sems[dop["stream"]], dop["stream"], dop["sig"])
                if op["dma"]:
                    k, pv = op["prev"]
                    wait(dsems[k], ("d", k), pv)
                ins = op["fn"](eng)
                if op["dma"]:
                    ins.then_inc(dsems[op["sig"][0]], 16)
                elif op["signal"]:
                    ins.then_inc(sems[sname], 1)
            if sname == "sp":
                for (sem, v) in final_dma:
                    eng.wait_ge(sem, v)

        @block.tensor
        def _(e):
            run_stream("pe", e)

        @block.scalar
        def _(e):
            run_stream("act", e)

        @block.vector
        def _(e):
            run_stream("dve", e)

        @block.gpsimd
        def _(e):
            run_stream("pool", e)

        @block.sync
        def _(e):
            run_stream("sp", e)

    def mm(self, out, lhsT, rhs, start=True, stop=True):
        return self.add("pe", lambda e: e.matmul(out, lhsT=lhsT, rhs=rhs, start=start, stop=stop),
                        [lhsT, rhs], [out])

    def tr(self, out, in_, ident):
        return self.add("pe", lambda e: e.transpose(out=out, in_=in_, identity=ident), [in_, ident], [out])

    def act(self, out, in_, func, bias=None, scale=None):
        kw = {}
        if bias is not None:
            kw["bias"] = bias
        if scale is not None:
            kw["scale"] = scale
        return self.add("act", lambda e: e.activation(out=out, in_=in_, func=func, **kw),
                        [in_, bias, scale], [out])

    def ts(self, eng, out, in0, s1, s2, op0, op1=None):
        if op1 is None:
            return self.add(eng, lambda e: e.tensor_scalar(out=out, in0=in0, scalar1=s1, scalar2=None, op0=op0),
                            [in0, s1], [out])
        return self.add(eng, lambda e: e.tensor_scalar(out=out, in0=in0, scalar1=s1, scalar2=s2, op0=op0, op1=op1),
                        [in0, s1, s2], [out])

    def tt(self, eng, out, in0, in1, op):
        return self.add(eng, lambda e: e.tensor_tensor(out=out, in0=in0, in1=in1, op=op), [in0, in1], [out])

    def stt(self, eng, out, in0, scalar, in1, op0, op1):
        return self.add(eng, lambda e: e.scalar_tensor_tensor(out=out, in0=in0, scalar=scalar, in1=in1,
                                                              op0=op0, op1=op1),
                        [in0, scalar, in1], [out])

    def copy(self, eng, out, in_):
        if eng == "act":
            return self.add("act", lambda e: e.copy(out=out, in_=in_), [in_], [out])
        return self.add(eng, lambda e: e.tensor_copy(out=out, in_=in_), [in_], [out])

    def memset(self, eng, out, val):
        return self.add(eng, lambda e: e.memset(out, val), [], [out])

    def dma(self, q, out, in_, slow=False):
        if slow:
            return self.add(q, lambda e: e.dma_start(out=out, in_=in_, allow_slow_non_contiguous=True),
                            [in_], [out], dma=True)
        return self.add(q, lambda e: e.dma_start(out=out, in_=in_), [in_], [out], dma=True)

    def gen(self, eng, fn, reads, writes):
        return self.add(eng, fn, reads, writes)


class TileDesc:
    pass


def build_program():
    nc = bass.Bass("TRN2", target_bir_lowering=False)

    def din(name, shape, dt=F32):
        return nc.dram_tensor(name, list(shape), dt, kind="ExternalInput").ap()

    def dout(name, shape, dt=F32):
        return nc.dram_tensor(name, list(shape), dt, kind="ExternalOutput").ap()

    def dint(name, shape, dt):
        return nc.dram_tensor(name, list(shape), dt, kind="Internal").ap()

    xp = din("xp", [NPS, SEQ, D])
    xs = din("xs", [NSS, SL, D])
    call = din("call", [NPS + NSS, D])
    spool = din("spool", [NSS, 15, PW])
    sC = din("sC", [NSS, NH, HD, HD])
    sn = din("sn", [NSS, NH, HD])
    sm = din("sm", [NSS, NH])
    w_ada = din("w_ada", [48, 128, 4096])
    b_ada = din("b_ada", [1, 6 * D])
    w_in = din("w_in", [52, 128, 4096])
    w_igfg_d = din("w_igfg", [128, 256])
    b_i = din("b_i", [1, NH])
    b_f = din("b_f", [1, NH])
    w_pool = din("w_pool", [4, 256, 256])
    pool_scale = din("pool_scale", [1, PW])
    gn_w = din("gn_w", [1, D])
    w_pa = din("w_pa", [8, 128, 2048])
    w_pb = din("w_pb", [8, 128, 4096])
    w_out = din("w_out", [8, 128, 4096])
    ln1_g = din("ln1_g", [1, D])
    ln1_b = din("ln1_b", [1, D])
    w_gu = din("w_gu", [DFF // 128, 128, 4096])
    w_down = din("w_down", [NFB * 8, 128, FBK * 256])
    ln2_g = din("ln2_g", [1, D])
    ln2_b = din("ln2_b", [1, D])

    yp = dout("yp", [NPS, SEQ, D])
    ys = dout("ys", [NSS, SL, D])
    o_pool_p = dout("pool_p", [NPS, 15, PW])
    o_C_p = dout("C_p", [NPS, NH, HD, HD])
    o_n_p = dout("n_p", [NPS, NH, HD])
    o_m_p = dout("m_p", [NPS, NH])
    o_pool_s = dout("pool_s", [NSS, 15, PW])
    o_C_s = dout("C_s", [NSS, NH, HD, HD])
    o_n_s = dout("n_s", [NSS, NH, HD])
    o_m_s = dout("m_s", [NSS, NH])

    wsc = {
        "in": dint("wsc_in", [52, 128, 4096], BF16),
        "pa": dint("wsc_pa", [8, 128, 2048], BF16),
        "pb": dint("wsc_pb", [8, 128, 4096], BF16),
        "out": dint("wsc_out", [8, 128, 4096], BF16),
        "gu": dint("wsc_gu", [DFF // 128, 128, 4096], BF16),
        "dn": dint("wsc_dn", [NFB * 8, 128, FBK * 256], BF16),
    }
    mod_d = dint("mod_d", [NPS + NSS, 6 * D], F32)

    es = ExitStack()
    with es:
        SB_BYTES = 207 * 1024
        S = es.enter_context(nc.sbuf_tensor("S", [128, SB_BYTES // 4], F32))
        Sb = S.bitcast(BF16)
        ps = [es.enter_context(nc.psum_tensor("ps%d" % i, [128, 512], F32)) for i in range(8)]
        psb = [p.bitcast(BF16) for p in ps]
        P = Prog(nc)

        cur = [0]

        def take(nbytes):
            o = cur[0]
            cur[0] += (nbytes + 63) // 64 * 64
            assert cur[0] <= SB_BYTES, ("SBUF overflow", cur[0])
            return o

        def V(off, shape, dt=F32):
            h = S if dt == F32 else Sb
            n = 1
            for s_ in shape[1:]:
                n *= s_
            b = off // _esz(dt)
            ap = h[0:shape[0], b:b + n]
            if len(shape) == 3:
                ap = ap.rearrange("p (a b) -> p a b", a=shape[1])
            elif len(shape) == 4:
                ap = ap.rearrange("p (a b c) -> p a b c", a=shape[1], b=shape[2])
            return ap

        def A(shape, dt=F32):
            n = _esz(dt)
            for s_ in shape[1:]:
                n *= s_
            return V(take(n), shape, dt)

        ident32 = A([128, 128])
        identb = A([128, 128], BF16)
        trineg = A([128, 128])
        onesneg = A([128, 128])
        ones32 = A([128, 128])
        sel128 = A([128, 128])
        sel32 = A([128, 128])
        caus = A([128, 128])
        maskT = A([128, 128])
        wpool = A([128, 4, 2, 256], BF16)
        wigfg = A([128, 16, 16], BF16)
        modT = A([128, NPS + NSS, 4, 16])
        psc = A([128, 8])
        gnw = A([128, 16])
        bif = A([128, 16])
        rc = A([128, 8, 16])
        Cst = A([128, NH, 2, 257])
        Cbf = A([128, 2, 2, 258], BF16)
        mst = A([128, NH])
        pcarry = A([128, 8, 16])
        bcs = [A([128, D]) for _ in range(2)]
        wslots = [take(SLOTB) for _ in range(NSLOT)]
        xres = A([128, NCH, D])
        phase0 = cur[0]
        uT = A([128, 16, TMAX], BF16)
        boutT = A([128, 16, TMAX], BF16)
        hT = boutT
        aoutT = A([128, 8, TMAX], BF16)
        PTB = 8 * NCH * (16 + LP) * 4
        r1sz = max(PTB + 16 * TMAX + 2 * 8 * (16 + LP) * 4, 32 * TMAX)
        r1 = take(r1sz)
        pT = V(r1, [128, 8, NCH, 16 + LP])
        yT = V(r1 + PTB, [128, 8, TMAX], BF16)
        poolA = V(r1 + PTB + 16 * TMAX, [128, 8, 16 + LP])
        poolB = V(r1 + PTB + 16 * TMAX + 8 * (16 + LP) * 4, [128, 8, 16 + LP])
        mergedT = V(r1, [128, 16, TMAX], BF16)
        T1off = r1
        T2off = r1 + NH * LP * 4
        save = cur[0]
        cur[0] = r1
        hb = []
        for _ in range(2):
            hb.append(dict(qT=A([128, 2, TMAX], BF16), ktok=A([128, NCH, 256], BF16), kT=A([128, 2, TMAX], BF16),
                           vaug=A([128, NCH, 258], BF16), sigo=A([128, NCH, 256], BF16),
                           kwb=A([128, NCH, 256], BF16)))
        assert cur[0] <= r1 + r1sz, (cur[0] - r1, r1sz)
        cur[0] = save
        dtx = take(max(NCH * NH * LP * 2, 2 * D * 2))
        DT = V(dtx, [128, NCH, NH, LP], BF16)
        xnb = [V(dtx + i * D * 2, [128, D], BF16) for i in range(2)]
        tmp0 = cur[0]
        tots = [A([128, 260]) for _ in range(2)]
        totA = [A([128, 260]) for _ in range(2)]
        hhs = [A([128, 256]) for _ in range(2)]
        hns = [A([128, 256], BF16) for _ in range(2)]
        kws = [A([128, 256], BF16) for _ in range(2)]
        sds = [A([128, LP], BF16) for _ in range(2)]
        sig_t = [A([128, 256]) for _ in range(2)]
        tmp_end = cur[0]
        cur[0] = tmp0
        sa_t = [A([128, TMAX]) for _ in range(2)]
        m1_t = [A([128, TMAX]) for _ in range(2)]
        tmp_end = max(tmp_end, cur[0])
        cur[0] = tmp0
        sil_t = [A([128, TMAX], BF16) for _ in range(2)]
        acc_t = [A([128, 256]) for _ in range(3)]
        tmp_end = max(tmp_end, cur[0])
        cur[0] = tmp0
        sptok = A([16, PW])
        pstok = A([16, PW])
        tmp_end = max(tmp_end, cur[0])
        cur[0] = tmp_end
        gv = {k: A([128, NCH, NH]) for k in ["ig", "g", "b", "mx", "mrow", "cneg", "winter", "expnm", "ws", "decay",
                                             "btot", "tdm", "tmp1", "tmp2", "tmp3"]}
        gx = A([128, 16])
        ge = A([128, NH])
        gl = A([128, NH])
        pmx = A([128, NH])
        gsl_t = [A([128, NCH, 256]) for _ in range(2)]
        sts = [dict(st=A([128, 4, 6]), mv=A([128, 2]), rstd=A([128, 1])) for _ in range(3)]
        gst = [dict(st=A([128, 6]), mv=A([128, 2]), rstd=A([128, 1]), den=A([128, 1]), rden=A([128, 1])) for _ in range(2)]
        small_end = cur[0]
        stg32 = [V(phase0 + i * 16384, [128, 4096]) for i in range(3)]
        stgb = [V(phase0 + 3 * 16384 + i * 8192, [128, 4096], BF16) for i in range(3)]
        assert phase0 + 3 * 16384 + 3 * 8192 <= SB_BYTES
        print("SBUF used", cur[0], "phase0", phase0)
        cur[0] = phase0 + 3 * 16384 + 3 * 8192
        cT = A([128, 16, 8])
        csil = A([128, 16, 8])
        brow = [A([1, 256]) for _ in range(2)]
        modsb = [A([8, 256]) for _ in range(2)]

        psi = [0]

        pspool = {"mode": "all", "c": 0, "p": 0}

        def PS(kind=None):
            if pspool["mode"] == "all":
                i = psi[0] % 8
                psi[0] += 1
                return i
            if kind == "proj":
                i = 5 + pspool["p"] % 3
                pspool["p"] += 1
                return i
            i = pspool["c"] % 5
            pspool["c"] += 1
            return i

        rr = {}

        def rot(key, n):
            v = rr.get(key, 0)
            rr[key] = v + 1
            return v % n

        def dbg(name, ap, dt=F32):
            if name not in DEBUG:
                return
            shp = list(ap.shape)
            dd = nc.dram_tensor("dbg_" + name, shp, dt, kind="ExternalOutput").ap()
            P.dma("pool", dd, ap)

        def aff(t, pattern, op, fill, base, cm):
            P.gen("pool", lambda e: e.affine_select(out=t, in_=t, pattern=pattern, compare_op=op, fill=fill,
                                                    base=base, channel_multiplier=cm), [t], [t])

        P.memset("pool", ident32, 1.0)
        aff(ident32, [[-1, 128]], ALU.is_equal, 0.0, 0, 1)
        P.copy("pool", identb, ident32)
        P.memset("pool", trineg, -1.0)
        aff(trineg, [[1, 128]], ALU.is_ge, 0.0, 0, -1)
        P.memset("pool", onesneg, -1.0)
        P.memset("pool", ones32, 1.0)
        P.memset("pool", sel128, 1.0)
        aff(sel128, [[0, 128]], ALU.is_equal, 0.0, -127, 1)
        P.memset("pool", sel32, 1.0)
        aff(sel32, [[0, 128]], ALU.is_equal, 0.0, -31, 1)
        P.memset("pool", caus, 0.0)
        aff(caus, [[-1, 128]], ALU.is_ge, -1e30, 0, 1)
        P.memset("pool", maskT, 0.0)
        aff(maskT, [[1, 128]], ALU.is_ge, -30000.0, 0, -1)
        for g_ in range(4):
            w_ = 2 << g_
            P.memset("pool", rc[:, 2 * g_:2 * g_ + 2, :], 1.0 / w_)
            for pos in range(w_ - 1):
                P.memset("pool", rc[:, 2 * g_:2 * g_ + 2, pos:pos + 1], 1.0 / (pos + 1))
        P.dma("sp", bif[:, 0:8], b_i.partition_broadcast(128).rearrange("p a b -> p (a b)"))
        P.dma("sp", bif[:, 8:16], b_f.partition_broadcast(128).rearrange("p a b -> p (a b)"))
        P.dma("sp", psc, pool_scale.rearrange("o (c p) -> p (o c)", p=128), slow=True)
        P.dma("sp", gnw, gn_w.rearrange("o (c p) -> p (o c)", p=128), slow=True)
        st0 = stg32[0]
        P.dma("sp", st0[:, 0:2048].rearrange("p (g c d) -> p g c d", g=4, c=2),
              w_pool.rearrange("g (c p) d -> p g c d", p=128))
        P.copy("dve", wpool, st0[:, 0:2048].rearrange("p (g c d) -> p g c d", g=4, c=2))
        st1 = stg32[1]
        P.dma("sp", st1[:, 0:256], w_igfg_d)
        P.copy("dve", wigfg, st1[:, 0:256].rearrange("p (k c) -> p k c", k=16))

        NSQ = NPS + NSS
        for s_ in range(NSQ):
            P.dma("sp", cT[:, :, s_], call[s_:s_ + 1, :].rearrange("o (k p) -> p (o k)", p=128), slow=True)
        P.act(csil[:, :, 0:NSQ], cT[:, :, 0:NSQ], AF.Silu)
        for u in range(48):
            sg = stg32[rot("stg", 3)]
            sgv = sg.rearrange("p (k c) -> p k c", k=16)
            P.dma("sp", sg, w_ada[u])
            br = brow[u % 2]
            P.dma("sp", br, b_ada[:, u * 256:(u + 1) * 256])
            pi = PS()
            for k in range(16):
                P.mm(ps[pi][:NSQ, 0:256], csil[:, k, 0:NSQ], sgv[:, k, :], start=(k == 0), stop=False)
            P.mm(ps[pi][:NSQ, 0:256], ones32[0:1, 0:NSQ], br[0:1, :], start=False, stop=True)
            mo = modsb[u % 2]
            P.copy("act", mo[:NSQ, :], ps[pi][:NSQ, 0:256])
            P.dma("pool", mod_d[:, u * 256:(u + 1) * 256], mo[:NSQ, :])
        for ki, koff in enumerate([0, D, 3 * D, 4 * D]):
            for s_ in range(NSQ):
                P.dma("sp", modT[:, s_, ki, :], mod_d[s_:s_ + 1, koff:koff + D].rearrange("o (k p) -> p (o k)", p=128),
                      slow=True)
        P.ts("dve", modT[:, :, 1, :], modT[:, :, 1, :], 1.0, None, ALU.add)
        P.ts("dve", modT[:, :, 3, :], modT[:, :, 3, :], 1.0, None, ALU.add)

        tiles = []
        for s_ in range(NPS):
            for ti in range(SEQ // TMAX):
                td = TileDesc()
                td.L = LP
                td.T = TMAX
                td.chunk_seq = [s_] * NCH
                td.groups = [(s_, list(range(NCH)))]
                td.x = [xp[s_, ti * TMAX + c * LP: ti * TMAX + (c + 1) * LP, :] for c in range(NCH)]
                td.y = [yp[s_, ti * TMAX + c * LP: ti * TMAX + (c + 1) * LP, :] for c in range(NCH)]
                td.start = [ti == 0 and c == 0 for c in range(NCH)]
                td.end = [ti == SEQ // TMAX - 1 and c == NCH - 1 for c in range(NCH)]
                td.sample = False
                td.first = (ti == 0)
                tiles.append(td)
        for ti in range(NSS // NCH):
            td = TileDesc()
            td.L = SL
            td.T = SL * NCH
            td.chunk_seq = [NPS + ti * NCH + c for c in range(NCH)]
            td.groups = [(NPS + ti * NCH + c, [c]) for c in range(NCH)]
            td.x = [xs[ti * NCH + c] for c in range(NCH)]
            td.y = [ys[ti * NCH + c] for c in range(NCH)]
            td.start = [False] * NCH
            td.end = [True] * NCH
            td.sample = True
            td.first = False
            tiles.append(td)

        if TILE_SEL is not None:
            tiles = [tiles[i] for i in TILE_SEL]

        def tile_units(td):
            u = [("in", i) for i in range(4)]
            for h in range(NH):
                u += [("in", 4 + h), ("in", 12 + h), ("in", 20 + h), ("in", 28 + h)]
            for jj in range(8):
                u += [("in", 36 + jj), ("pa", jj), ("in", 44 + jj), ("pb", jj)]
            u += [("out", cs) for cs in range(8)]
            for fb in range(NFB):
                u += [("gu", fb * FBK + j) for j in range(FBK)]
                u += [("dn", fb * 8 + cs) for cs in range(8)]
            return u

        order = []
        for td in tiles:
            order += tile_units(td)
        usz = {"in": 4096, "pa": 2048, "pb": 4096, "out": 4096, "gu": 4096, "dn": FBK * 256}
        wstate = dict(i=0, loaded=0)

        wfp = {"in": w_in, "pa": w_pa, "pb": w_pb, "out": w_out, "gu": w_gu, "dn": w_down}

        n_first = len(tile_units(tiles[0]))

        def wload(j):
            kind, idx = order[j]
            n = usz[kind]
            slot = wslots[j % NSLOT]
            if j < n_first:
                P.dma("pool", V(slot, [128, n], BF16), wfp[kind][idx])
                P.dma("sp", wsc[kind][idx], V(slot, [128, n], BF16))
            else:
                P.dma("sp", V(slot, [128, n], BF16), wsc[kind][idx])

        def wget(key):
            i = wstate["i"]
            assert order[i] == key, (order[i], key, i)
            while wstate["loaded"] < min(len(order), i + NSLOT):
                wload(wstate["loaded"])
                wstate["loaded"] += 1
            wstate["i"] = i + 1
            return wslots[i % NSLOT]

        def bcload(slot, row_ap):
            P.dma("pool", bcs[slot], row_ap.partition_broadcast(128).rearrange("p a b -> p (a b)"))

        def ln_stats(src, L):
            s_ = sts[rot("sts", 3)]
            for q in range(4):
                P.gen("dve", lambda e, q=q: e.bn_stats(out=s_["st"][:L, q, :], in_=src[:, q * 512:(q + 1) * 512]),
                      [src[:, q * 512:(q + 1) * 512]], [s_["st"][:L, q, :]])
            P.gen("dve", lambda e: e.bn_aggr(out=s_["mv"][:L, :], in_=s_["st"][:L, :, :]), [s_["st"][:L, :, :]],
                  [s_["mv"][:L, :]])
            P.act(s_["rstd"][:L, :], s_["mv"][:L, 1:2], AF.Ln, bias=EPS)
            P.act(s_["rstd"][:L, :], s_["rstd"][:L, :], AF.Exp, scale=-0.5)
            return s_["mv"][:L, 0:1], s_["rstd"][:L, :]

        def ln_to_T(src, L, c, seq, ksh, ksc):
            if LNCUT < 1:
                return
            mean, rstd = ln_stats(src, L)
            if LNCUT < 2:
                return
            xb = xnb[rot("xnb", 2)]
            P.ts("dve", xb[:L, :], src, mean, rstd, ALU.subtract, ALU.mult)
            if LNCUT < 3:
                return
            for g4 in range(4):
                pi = PS()
                for j in range(4):
                    fc = g4 * 4 + j
                    P.tr(psb[pi][:, j * L:(j + 1) * L], xb[:L, fc * 128:(fc + 1) * 128], identb[:L, :L])
                if LNCUT < 4:
                    continue
                for j in range(4):
                    fc = g4 * 4 + j
                    dst = uT[:, fc, c * L:(c + 1) * L]
                    src_ps = psb[pi][:, j * L:(j + 1) * L]
                    if (EVAC == 'act') or (EVAC == 'mix' and (fc % 2) == 0):
                        P.act(dst, src_ps, AF.Identity, bias=modT[:, seq, ksh, fc:fc + 1],
                              scale=modT[:, seq, ksc, fc:fc + 1])
                    else:
                        P.ts("dve", dst, src_ps, modT[:, seq, ksc, fc:fc + 1], modT[:, seq, ksh, fc:fc + 1],
                             ALU.mult, ALU.add)

        def ln_affine(dst, L, gslot, bslot):
            mean, rstd = ln_stats(dst, L)
            P.ts("dve", dst, dst, mean, rstd, ALU.subtract, ALU.mult)
            P.tt("dve", dst, dst, bcs[gslot][:L, :], ALU.mult)
            P.tt("dve", dst, dst, bcs[bslot][:L, :], ALU.add)

        for tix, td in enumerate(tiles):
            L = td.L
            T = td.T
            HPM = min(NH, 512 // L)
            sel = sel128 if L == 128 else sel32
            T1 = V(T1off, [128, NH, L])
            T2 = V(T2off, [128, NH, L])

            for c in range(NCH):
                P.dma("pool", xres[:L, c, :], td.x[c])
            for c in range(NCH):
                ln_to_T(xres[:L, c, :], L, c, td.chunk_seq[c], 0, 1)
            if tix == 0:
                dbg("uT", uT[:, :, 0:T], BF16)

            dbg("mst_beg%d" % tix, mst)
            if (td.sample or STOP_ALL) and STOP_AT == "S2":
                continue
            for c in range(NCH):
                seq = td.chunk_seq[c]
                if td.start[c]:
                    P.memset("pool", mst, 0.0)
                    P.memset("pool", Cst, 0.0)
                    P.memset("pool", pcarry, 0.0)
                if td.sample:
                    P.dma("pool", mst, sm[seq - NPS:seq - NPS + 1, :].partition_broadcast(128).rearrange("p a b -> p (a b)"))
                G = {k: v[:, c, :] for k, v in gv.items()}
                pi = PS()
                for k in range(16):
                    P.mm(ps[pi][:L, 0:16], uT[:, k, c * L:(c + 1) * L], wigfg[:, k, :], start=(k == 0), stop=(k == 15))
                P.tt("dve", gx[:L, :], ps[pi][:L, 0:16], bif[:L, :], ALU.add)
                P.copy("dve", G["ig"][:L, :], gx[:L, 0:8])
                P.ts("dve", gx[:L, 8:16], gx[:L, 8:16], -50.0, None, ALU.max)
                P.act(ge[:L, :], gx[:L, 8:16], AF.Exp, scale=-1.0)
                P.act(gl[:L, :], ge[:L, :], AF.Ln, bias=1.0)
                pi = PS()
                P.mm(ps[pi][:L, 0:8], trineg[:L, :L], gl[:L, :])
                P.mm(ps[pi][:128, 8:16], onesneg[:L, :128], gl[:L, :])
                P.copy("act", G["b"][:L, :], ps[pi][:L, 0:8])
                P.copy("act", G["btot"], ps[pi][:, 8:16])
                P.tt("dve", G["g"][:L, :], G["ig"][:L, :], G["b"][:L, :], ALU.subtract)
                P.tt("dve", T1[:L, :, :], ident32[:L, :L].unsqueeze(1).to_broadcast([L, NH, L]),
                     G["g"][:L, :].unsqueeze(2).to_broadcast([L, NH, L]), ALU.mult)
                for hq in range(NH // HPM):
                    pi = PS()
                    P.mm(ps[pi][:L, 0:HPM * L], ones32[:L, :L],
                         T1[:L, hq * HPM:(hq + 1) * HPM, :].rearrange("p a b -> p (a b)"))
                    P.tt("dve", T2[:L, hq * HPM:(hq + 1) * HPM, :],
                         ps[pi][:L, 0:HPM * L].rearrange("p (a b) -> p a b", a=HPM),
                         caus[:L, :L].unsqueeze(1).to_broadcast([L, HPM, L]), ALU.add)
                P.gen("dve", lambda e, L=L, T2=T2: e.tensor_reduce(out=pmx[:L, :], in_=T2[:L, :, :], axis=AX.X, op=ALU.max),
                      [T2[:L, :, :]], [pmx[:L, :]])
                P.tt("dve", G["mx"][:L, :], pmx[:L, :], mst[:L, :], ALU.max)
                P.tt("dve", G["mrow"][:L, :], G["b"][:L, :], G["mx"][:L, :], ALU.add)
                P.ts("dve", G["cneg"][:L, :], G["mx"][:L, :], -1.0, None, ALU.mult)
                P.tt("dve", G["tmp1"][:L, :], mst[:L, :], G["mx"][:L, :], ALU.subtract)
                P.act(G["winter"][:L, :], G["tmp1"][:L, :], AF.Exp)
                P.act(G["expnm"][:L, :], G["mrow"][:L, :], AF.Exp, scale=-1.0)
                P.tt("dve", T1[:L, :, :], ident32[:L, :L].unsqueeze(1).to_broadcast([L, NH, L]),
                     G["cneg"][:L, :].unsqueeze(2).to_broadcast([L, NH, L]), ALU.mult)
                for hq in range(NH // HPM):
                    pi = PS()
                    P.mm(ps[pi][:L, 0:HPM * L], ones32[:L, :L],
                         T1[:L, hq * HPM:(hq + 1) * HPM, :].rearrange("p a b -> p (a b)"))
                    P.tt("dve", T2[:L, hq * HPM:(hq + 1) * HPM, :],
                         ps[pi][:L, 0:HPM * L].rearrange("p (a b) -> p a b", a=HPM),
                         maskT[:L, :L].unsqueeze(1).to_broadcast([L, HPM, L]), ALU.add)
                for h in range(NH):
                    P.act(DT[:L, c, h, :L], T2[:L, h, :], AF.Exp, bias=G["g"][:L, h:h + 1])
                pi = PS()
                P.mm(ps[pi][:128, 0:8], sel[:L, :128], G["mrow"][:L, :])
                P.tt("dve", G["tdm"], G["btot"], ps[pi][:, 0:8], ALU.subtract)
                P.tt("dve", G["tmp2"][:L, :], G["g"][:L, :], G["tdm"][:L, :], ALU.add)
                P.act(G["ws"][:L, :], G["tmp2"][:L, :], AF.Exp)
                P.tt("dve", G["tmp3"], G["tdm"], mst, ALU.add)
                P.act(G["decay"], G["tmp3"], AF.Exp)
                P.copy("dve", mst, ps[pi][:, 0:8])
                if td.end[c]:
                    if td.sample:
                        P.dma("pool", o_m_s[seq - NPS:seq - NPS + 1, :], mst[0:1, :])
                    else:
                        P.dma("pool", o_m_p[seq:seq + 1, :], mst[0:1, :])
                if tix == 0 and c == 0:
                    dbg("DT", DT[:, 0, :, :], BF16)
                    dbg("mrow", gv["mrow"][:, 0, :])
                    dbg("gb", gv["b"][:, 0, :])

            dbg("mst_end%d" % tix, mst)
            dbg("mrowA%d" % tix, gv["mrow"][:, 0, :])
            dbg("mrowB%d" % tix, gv["mrow"][:, 1, :])
            dbg("mxB%d" % tix, gv["mx"][:, 1, :])
            dbg("bB%d" % tix, gv["b"][:, 1, :])
            if tix == 0:
                dbg("mrow1", gv["mrow"][:, 1, :])
                dbg("btot1", gv["btot"][:, 1, :])
                dbg("mx1", gv["mx"][:, 1, :])

            if (td.sample or STOP_ALL) and STOP_AT == "S3":
                continue
            for i in range(4):
                wo = wget(("in", i))
                wv = V(wo, [128, 16, 256], BF16)
                for half in range(2):
                    cc = 2 * i + half
                    pi = PS()
                    for k in range(16):
                        P.mm(ps[pi][:, 0:T], wv[:, k, half * 128:(half + 1) * 128], uT[:, k, 0:T],
                             start=(k == 0), stop=(k == 15))
                    P.copy("act", pT[:, cc, :, 16:16 + L], ps[pi][:, 0:T].rearrange("p (a b) -> p a b", a=NCH))
            for c in range(NCH):
                seq = td.chunk_seq[c]
                W_ = 16 + L
                if td.sample:
                    P.dma("pool", sptok[0:15, :], spool[seq - NPS])
                    pi = PS()
                    for cc in range(8):
                        P.tr(ps[pi][:, cc * 16 + 1:cc * 16 + 16], sptok[0:15, cc * 128:(cc + 1) * 128],
                             ident32[0:15, 0:15])
                    P.copy("dve", pT[:, :, c, 1:16],
                           ps[pi][:, 0:128].rearrange("p (a b) -> p a b", a=8)[:, :, 1:16])
                elif c == 0:
                    P.copy("pool", pT[:, :, 0, 1:16], pcarry[:, :, 1:16])
                else:
                    P.copy("pool", pT[:, :, c, 1:16], pT[:, :, c - 1, L + 1:L + 16])
                Pc = pT[:, :, c, :]
                P.tt("dve", poolA[:, :, 2:W_], Pc[:, :, 2:W_], Pc[:, :, 1:W_ - 1], ALU.add)
                P.tt("dve", poolB[:, 2:8, 4:W_], poolA[:, 2:8, 4:W_], poolA[:, 2:8, 2:W_ - 2], ALU.add)
                P.tt("dve", poolA[:, 4:8, 8:W_], poolB[:, 4:8, 8:W_], poolB[:, 4:8, 4:W_ - 4], ALU.add)
                P.tt("dve", poolB[:, 6:8, 16:W_], poolA[:, 6:8, 16:W_], poolA[:, 6:8, 8:W_ - 8], ALU.add)
                for g_ in range(4):
                    src = (poolA if g_ % 2 == 0 else poolB)
                    P.stt("dve", yT[:, 2 * g_:2 * g_ + 2, c * L:(c + 1) * L], src[:, 2 * g_:2 * g_ + 2, 16:W_],
                          1.0 / (2 << g_), Pc[:, 2 * g_:2 * g_ + 2, 16:W_], ALU.mult, ALU.subtract)
                    if td.start[c]:
                        P.tt("dve", src[:, 2 * g_:2 * g_ + 2, 16:32], src[:, 2 * g_:2 * g_ + 2, 16:32],
                             rc[:, 2 * g_:2 * g_ + 2, :], ALU.mult)
                        P.tt("dve", yT[:, 2 * g_:2 * g_ + 2, c * L:c * L + 16], src[:, 2 * g_:2 * g_ + 2, 16:32],
                             Pc[:, 2 * g_:2 * g_ + 2, 16:32], ALU.subtract)
                if td.end[c]:
                    for hq in range(2):
                        pi = PS()
                        for q in range(4):
                            cc = hq * 4 + q
                            P.tr(ps[pi][0:15, q * 128:(q + 1) * 128], pT[:, cc, c, L + 1:L + 16], ident32[:, :])
                        P.copy("act", pstok[0:15, hq * 512:(hq + 1) * 512], ps[pi][0:15, 0:512])
                    if td.sample:
                        P.dma("pool", o_pool_s[seq - NPS], pstok[0:15, :])
                    else:
                        P.dma("pool", o_pool_p[seq], pstok[0:15, :])
            if not td.sample:
                P.copy("pool", pcarry[:, :, 1:16], pT[:, :, NCH - 1, L + 1:L + 16])
            if tix == 0:
                dbg("yT", yT[:, :, 0:T], BF16)
            for g_ in range(4):
                for dc in range(2):
                    pi = PS()
                    for k in range(2):
                        P.mm(ps[pi][:, 0:T], wpool[:, g_, k, dc * 128:(dc + 1) * 128], yT[:, 2 * g_ + k, 0:T],
                             start=(k == 0), stop=(k == 1))
                    P.ts("dve", aoutT[:, 2 * g_ + dc, 0:T], ps[pi][:, 0:T], psc[:, 2 * g_ + dc:2 * g_ + dc + 1], None,
                         ALU.mult)
            if tix == 0:
                dbg("aoutT", aoutT[:, :, 0:T], BF16)

            if (td.sample or STOP_ALL) and STOP_AT == "S4":
                continue
            if len(td.groups) == 1:
                bcload(0, mod_d[td.groups[0][0]:td.groups[0][0] + 1, 2 * D:3 * D])
            def proj_jobs(h):
                B_ = hb[h % 2]
                qT, ktok, kT, vaug, sigo, kwb = B_["qT"], B_["ktok"], B_["kT"], B_["vaug"], B_["sigo"], B_["kwb"]
                jobs = []
                st = {}

                def unit(key, shape):
                    if key not in st:
                        st[key] = V(wget(key), shape, BF16)
                    return st[key]

                for dc in range(2):
                    def mmq(dc=dc):
                        wv = unit(("in", 4 + h), [128, 16, 256])
                        pi = PS("proj")
                        for k in range(16):
                            P.mm(ps[pi][:, 0:T], wv[:, k, dc * 128:(dc + 1) * 128], uT[:, k, 0:T],
                                 start=(k == 0), stop=(k == 15))
                        return pi

                    def evq(pi, dc=dc):
                        P.copy("dve", qT[:, dc, 0:T], ps[pi][:, 0:T])
                    jobs.append((mmq, evq))
                for c in range(NCH):
                    def mmk(c=c):
                        wv = unit(("in", 12 + h), [128, 16, 256])
                        pi = PS("proj")
                        for k in range(16):
                            P.mm(ps[pi][:L, 0:256], uT[:, k, c * L:(c + 1) * L], wv[:, k, :], start=(k == 0), stop=(k == 15))
                        return pi

                    def evk(pi, c=c):
                        P.act(ktok[:L, c, :], ps[pi][:L, 0:256], AF.Identity, scale=1.0 / 16.0)
                    jobs.append((mmk, evk))
                for c in range(NCH):
                    def mmt(c=c):
                        pj = PS("proj")
                        for dc in range(2):
                            P.tr(psb[pj][:, dc * L:(dc + 1) * L], ktok[:L, c, dc * 128:(dc + 1) * 128], identb[:L, :L])
                        return pj

                    def evt(pj, c=c):
                        P.copy("dve", kT[:, :, c * L:(c + 1) * L],
                               psb[pj][:, 0:2 * L].rearrange("p (a b) -> p a b", a=2))
                        P.ts("dve", kwb[:L, c, :], ktok[:L, c, :], gv["ws"][:L, c, h:h + 1], None, ALU.mult)
                    jobs.append((mmt, evt))
                for c in range(NCH):
                    def mmv(c=c):
                        wv = unit(("in", 20 + h), [128, 16, 256])
                        pi = PS("proj")
                        for k in range(16):
                            P.mm(ps[pi][:L, 0:256], uT[:, k, c * L:(c + 1) * L], wv[:, k, :], start=(k == 0), stop=(k == 15))
                        return pi

                    def evv(pi, c=c):
                        P.copy("dve", vaug[:L, c, 0:256], ps[pi][:L, 0:256])
                        P.memset("pool", vaug[:L, c, 256:257], 1.0)
                    jobs.append((mmv, evv))
                for c in range(NCH):
                    def mmo(c=c):
                        wv = unit(("in", 28 + h), [128, 16, 256])
                        pi = PS("proj")
                        for k in range(16):
                            P.mm(ps[pi][:L, 0:256], uT[:, k, c * L:(c + 1) * L], wv[:, k, :], start=(k == 0), stop=(k == 15))
                        return pi

                    def evo(pi, c=c):
                        se = sig_t[rot("sig", 2)]
                        P.act(se[:L, :], ps[pi][:L, 0:256], AF.Exp, scale=-1.0)
                        P.act(se[:L, :], se[:L, :], AF.Ln, bias=1.0)
                        P.act(sigo[:L, c, :], se[:L, :], AF.Exp, scale=-1.0)
                    jobs.append((mmo, evo))
                return jobs

            pspool["mode"] = "split"
            SPF = 4

            def sslot(h_, c_):
                return (h_ * NCH + c_) % NH if td.sample else h_

            def sload(step):
                h_, c_ = step // NCH, step % NCH
                if h_ >= NH:
                    return
                sq = td.chunk_seq[c_] - NPS
                sl_ = sslot(h_, c_)
                P.dma("pool", Cst[:, sl_, :, 0:256], sC[sq, h_].rearrange("(c p) e -> p c e", p=128))
                nrow = gsl_t[0][0:2, step % NCH, 0:128]
                P.dma("pool", nrow, sn[sq, h_].rearrange("(c p) -> c p", p=128))

            if td.sample:
                for st_ in range(SPF):
                    sload(st_)
            for (mmf, evf) in proj_jobs(0):
                evf(mmf())
            for h in range(NH):
                B_ = hb[h % 2]
                qT, ktok, kT, vaug, sigo, kwb = B_["qT"], B_["ktok"], B_["kT"], B_["vaug"], B_["sigo"], B_["kwb"]
                nxt = list(proj_jobs(h + 1)) if h + 1 < NH else []
                per_chunk = -(-len(nxt) // NCH)

                if not td.sample:
                    P.copy("act", Cbf[:, h % 2, :, 0:257], Cst[:, h, :, :])
                for c in range(NCH):
                    seq = td.chunk_seq[c]
                    G = {k: v[:, c, :] for k, v in gv.items()}
                    hs = sslot(h, c)
                    if td.sample:
                        nrow = gsl_t[0][0:2, (h * NCH + c) % NCH, 0:128]
                        pn = PS()
                        P.tr(ps[pn][:, 0:2], nrow, ident32[0:2, 0:2])
                        P.copy("act", Cst[:, hs, :, 256], ps[pn][:, 0:2])
                        sload(h * NCH + c + SPF)
                        P.copy("act", Cbf[:, h % 2, :, 0:257], Cst[:, hs, :, :])
                    pS = PS()
                    for dc in range(2):
                        P.mm(ps[pS][:L, 0:L], kT[:, dc, c * L:(c + 1) * L], qT[:, dc, c * L:(c + 1) * L],
                             start=(dc == 0), stop=(dc == 1))
                    sd = sds[rot("sd", 2)]
                    P.tt("dve", sd[:L, :L], ps[pS][:L, 0:L], DT[:L, c, h, :L], ALU.mult)
                    pA = PS()
                    P.mm(ps[pA][:L, 0:257], sd[:L, :L], vaug[:L, c, 0:257])
                    pB = PS()
                    for dc in range(2):
                        P.mm(ps[pB][:L, 0:257], qT[:, dc, c * L:(c + 1) * L], Cbf[:, h % 2, dc, 0:257],
                             start=(dc == 0), stop=(dc == 1))
                    pUs = []
                    for dc in range(2):
                        pU = PS()
                        P.mm(ps[pU][:, 0:257], kwb[:L, c, dc * 128:(dc + 1) * 128], vaug[:L, c, 0:257])
                        pUs.append(pU)
                    todo = nxt[:per_chunk]
                    nxt = nxt[per_chunk:]
                    deferred = [(evf, mmf()) for (mmf, evf) in todo[:3]]
                    for dc in range(2):
                        P.stt("dve", Cst[:, hs, dc, :], Cst[:, hs, dc, :], G["decay"][:, h:h + 1], ps[pUs[dc]][:, 0:257],
                              ALU.mult, ALU.add)
                    P.copy("act", Cbf[:, h % 2, :, 0:257], Cst[:, hs, :, :])
                    tA = totA[rot("totA", 2)]
                    tot = tots[rot("tot", 2)]
                    P.copy("act", tA[:L, 0:257], ps[pA][:L, 0:257])
                    P.stt("dve", tot[:L, 0:257], ps[pB][:L, 0:257], G["winter"][:L, h:h + 1], tA[:L, 0:257],
                          ALU.mult, ALU.add)
                    gs = gst[rot("gst", 2)]
                    P.stt("dve", gs["den"][:L, :], tot[:L, 256:257], -1.0, tot[:L, 256:257], ALU.mult, ALU.max)
                    P.tt("dve", gs["den"][:L, :], gs["den"][:L, :], G["expnm"][:L, h:h + 1], ALU.max)
                    P.gen("dve", lambda e, gs=gs, L=L: e.reciprocal(out=gs["rden"][:L, :], in_=gs["den"][:L, :]),
                          [gs["den"][:L, :]], [gs["rden"][:L, :]])
                    hh = hhs[rot("hh", 2)]
                    P.stt("dve", hh[:L, :], tot[:L, 0:256], gs["rden"][:L, :], sigo[:L, c, :], ALU.mult, ALU.mult)
                    P.gen("dve", lambda e, gs=gs, hh=hh, L=L: e.bn_stats(out=gs["st"][:L, :], in_=hh[:L, :]),
                          [hh[:L, :]], [gs["st"][:L, :]])
                    P.gen("dve", lambda e, gs=gs, L=L: e.bn_aggr(out=gs["mv"][:L, :], in_=gs["st"][:L, :]),
                          [gs["st"][:L, :]], [gs["mv"][:L, :]])
                    P.act(gs["rstd"][:L, :], gs["mv"][:L, 1:2], AF.Ln, bias=EPS)
                    P.act(gs["rstd"][:L, :], gs["rstd"][:L, :], AF.Exp, scale=-0.5)
                    hn = hns[rot("hn", 2)]
                    P.ts("dve", hn[:L, :], hh[:L, :], gs["mv"][:L, 0:1], gs["rstd"][:L, :], ALU.subtract, ALU.mult)
                    pj = PS()
                    for dc in range(2):
                        P.tr(psb[pj][:, dc * L:(dc + 1) * L], hn[:L, dc * 128:(dc + 1) * 128], identb[:L, :L])
                    for dc in range(2):
                        P.act(boutT[:, 2 * h + dc, c * L:(c + 1) * L], psb[pj][:, dc * L:(dc + 1) * L], AF.Identity,
                              scale=gnw[:, 2 * h + dc:2 * h + dc + 1])
                    if td.end[c]:
                        oC = o_C_s[seq - NPS, h] if td.sample else o_C_p[seq, h]
                        on = o_n_s[seq - NPS, h:h + 1, :] if td.sample else o_n_p[seq, h:h + 1, :]
                        P.dma("pool", oC.rearrange("(c p) e -> p c e", p=128), Cst[:, hs, :, 0:256])
                        pn = PS()
                        P.tr(ps[pn][0:2, 0:128], Cst[:, hs, :, 256], ident32[:, :])
                        nro = gsl_t[1][0:2, rot("nro", NCH), 0:128]
                        P.copy("act", nro, ps[pn][0:2, 0:128])
                        P.dma("pool", on[0].rearrange("(c p) -> c p", p=128), nro)
                    for (evf, pi_) in deferred:
                        evf(pi_)
                    for (mmf, evf) in todo[3:]:
                        evf(mmf())
                    if tix == 0 and c == 0 and h == 0:
                        dbg("hh", hh[:, :])
                        dbg("tot", tot[:, 0:257])
                for (mmf, evf) in nxt:
                    evf(mmf())
            pspool["mode"] = "all"
            if tix == 0:
                dbg("boutT", boutT[:, :, 0:T], BF16)

            if (td.sample or STOP_ALL) and STOP_AT == "S5":
                continue
            bcload(1, ln1_g)
            for jj in range(8):
                wg = V(wget(("in", 36 + jj)), [128, 16, 256], BF16)
                sas = []
                for half in range(2):
                    pi = PS()
                    for k in range(16):
                        P.mm(ps[pi][:, 0:T], wg[:, k, half * 128:(half + 1) * 128], uT[:, k, 0:T],
                             start=(k == 0), stop=(k == 15))
                    sa = sa_t[half]
                    P.act(sa[:, 0:T], ps[pi][:, 0:T], AF.Sigmoid)
                    sas.append(sa)
                wa = V(wget(("pa", jj)), [128, 8, 256], BF16)
                m1s = []
                for half in range(2):
                    pi = PS()
                    for k in range(8):
                        P.mm(ps[pi][:, 0:T], wa[:, k, half * 128:(half + 1) * 128], aoutT[:, k, 0:T],
                             start=(k == 0), stop=(k == 7))
                    m1 = m1_t[half]
                    P.tt("dve", m1[:, 0:T], sas[half][:, 0:T], ps[pi][:, 0:T], ALU.mult)
                    m1s.append(m1)
                wg = V(wget(("in", 44 + jj)), [128, 16, 256], BF16)
                sbs = []
                for half in range(2):
                    pi = PS()
                    for k in range(16):
                        P.mm(ps[pi][:, 0:T], wg[:, k, half * 128:(half + 1) * 128], uT[:, k, 0:T],
                             start=(k == 0), stop=(k == 15))
                    sb_ = sa_t[half]
                    P.act(sb_[:, 0:T], ps[pi][:, 0:T], AF.Sigmoid)
                    sbs.append(sb_)
                wb = V(wget(("pb", jj)), [128, 16, 256], BF16)
                for half in range(2):
                    pi = PS()
                    for k in range(16):
                        P.mm(ps[pi][:, 0:T], wb[:, k, half * 128:(half + 1) * 128], boutT[:, k, 0:T],
                             start=(k == 0), stop=(k == 15))
                    P.tt("dve", sbs[half][:, 0:T], sbs[half][:, 0:T], ps[pi][:, 0:T], ALU.mult)
                    P.tt("dve", mergedT[:, 2 * jj + half, 0:T], m1s[half][:, 0:T], sbs[half][:, 0:T], ALU.add)
            if tix == 0:
                dbg("mergedT", mergedT[:, :, 0:T], BF16)

            if (td.sample or STOP_ALL) and STOP_AT == "S6":
                continue
            multi = len(td.groups) > 1
            s0 = td.chunk_seq[0]

            def gate_rows(koff, cs, slot):
                if not multi:
                    return [bcs[slot][:L, cs * 256:(cs + 1) * 256]] * NCH
                gsl = gsl_t[rot("gsl", 2)]
                P.dma("pool", gsl[:L, :, :],
                      mod_d[s0:s0 + NCH, koff + cs * 256:koff + (cs + 1) * 256].partition_broadcast(L))
                return [gsl[:L, c, :] for c in range(NCH)]

            if True:
                for cs in range(8):
                    wv = V(wget(("out", cs)), [128, 16, 256], BF16)
                    grow = gate_rows(2 * D, cs, 0)
                    for c in range(NCH):
                        pi = PS()
                        for k in range(16):
                            P.mm(ps[pi][:L, 0:256], mergedT[:, k, c * L:(c + 1) * L], wv[:, k, :],
                                 start=(k == 0), stop=(k == 15))
                        ac = acc_t[rot("acc", 3)]
                        P.tt("dve", ac[:L, :], ps[pi][:L, 0:256], grow[c], ALU.mult)
                        xs_ = xres[:L, c, cs * 256:(cs + 1) * 256]
                        P.stt("dve", xs_, xs_, ALPHA, ac[:L, :], ALU.mult, ALU.add)
            bcload(0, ln1_b)
            for c in range(NCH):
                ln_affine(xres[:L, c, :], L, 1, 0)
            if tix == 0:
                dbg("x1", xres[:, 0, :])
            if len(td.groups) == 1:
                bcload(1, mod_d[td.groups[0][0]:td.groups[0][0] + 1, 5 * D:6 * D])
            for c in range(NCH):
                ln_to_T(xres[:L, c, :], L, c, td.chunk_seq[c], 2, 3)
            bcload(0, ln2_g)

            if (td.sample or STOP_ALL) and STOP_AT == "S8":
                continue
            for fb in range(NFB):
                for j in range(FBK):
                    wv = V(wget(("gu", fb * FBK + j)), [128, 2, 16, 128], BF16)
                    pg = PS()
                    for k in range(16):
                        P.mm(ps[pg][:, 0:T], wv[:, 0, k, :], uT[:, k, 0:T], start=(k == 0), stop=(k == 15))
                    pu = PS()
                    for k in range(16):
                        P.mm(ps[pu][:, 0:T], wv[:, 1, k, :], uT[:, k, 0:T], start=(k == 0), stop=(k == 15))
                    sl = sil_t[rot("sil", 2)]
                    P.act(sl[:, 0:T], ps[pg][:, 0:T], AF.Silu)
                    P.tt("dve", hT[:, j, 0:T], sl[:, 0:T], ps[pu][:, 0:T], ALU.mult)
                if True:
                    for cs in range(8):
                        wv = V(wget(("dn", fb * 8 + cs)), [128, FBK, 256], BF16)
                        grow = gate_rows(5 * D, cs, 1)
                        for c in range(NCH):
                            pi = PS()
                            for k in range(FBK):
                                P.mm(ps[pi][:L, 0:256], hT[:, k, c * L:(c + 1) * L], wv[:, k, :],
                                     start=(k == 0), stop=(k == FBK - 1))
                            ac = acc_t[rot("acc", 3)]
                            P.tt("dve", ac[:L, :], ps[pi][:L, 0:256], grow[c], ALU.mult)
                            xs_ = xres[:L, c, cs * 256:(cs + 1) * 256]
                            if fb == 0:
                                P.stt("dve", xs_, xs_, ALPHA, ac[:L, :], ALU.mult, ALU.add)
                            else:
                                P.tt("dve", xs_, xs_, ac[:L, :], ALU.add)
            bcload(1, ln2_b)
            for c in range(NCH):
                ln_affine(xres[:L, c, :], L, 0, 1)
                P.dma("pool", td.y[c], xres[:L, c, :])

        assert STOP_AT is not None or wstate["i"] == len(order)
        P.emit(es)
    return nc


def kernel(x_prompt, x_sample, c_prompt, c_sample, state_pool, state_mlstm_C, state_mlstm_n, state_mlstm_m,
           w_ada, b_ada, w_in, b_i, b_f, w_pool, pool_scale, gn_w, w_pa, w_pb, w_out, ln1_g, ln1_b,
           w_gate, w_up, w_down, ln2_g, ln2_b):
    n = 8
    f = lambda a: np.ascontiguousarray(np.asarray(a, dtype=np.float32))
    def blk(w, c0, nc_, r0=0, nr=None):
        w = np.asarray(w)
        nr = w.shape[0] - r0 if nr is None else nr
        return w[r0:r0 + nr, c0:c0 + nc_].reshape(nr // 128, 128, nc_).transpose(1, 0, 2)

    wi = np.asarray(w_in[0], dtype=np.float32)
    in_cols = [256 * i for i in range(36)] + [COL_GA + 256 * i for i in range(8)] + [COL_GB + 256 * i for i in range(8)]
    wg_, wu_, wd_ = (np.asarray(a[0], dtype=np.float32) for a in (w_gate, w_up, w_down))
    shared = {
        "w_ada": f(np.stack([blk(w_ada[0], 256 * u, 256).reshape(128, 4096) for u in range(48)])),
        "w_in": f(np.stack([blk(wi, c0, 256).reshape(128, 4096) for c0 in in_cols])),
        "w_igfg": f(blk(wi, COL_IG, 16).reshape(128, 256)),
        "w_pa": f(np.stack([blk(w_pa[0], 256 * i, 256).reshape(128, 2048) for i in range(8)])),
        "w_pb": f(np.stack([blk(w_pb[0], 256 * i, 256).reshape(128, 4096) for i in range(8)])),
        "w_out": f(np.stack([blk(w_out[0], 256 * i, 256).reshape(128, 4096) for i in range(8)])),
        "w_gu": f(np.stack([np.concatenate([blk(wg_, 128 * j, 128).reshape(128, 2048),
                                            blk(wu_, 128 * j, 128).reshape(128, 2048)], axis=1)
                            for j in range(DFF // 128)])),
        "w_down": f(np.stack([blk(wd_, 256 * cs, 256, fb * FBK * 128, FBK * 128).reshape(128, FBK * 256)
                              for fb in range(NFB) for cs in range(8)])),
        "b_ada": f(b_ada), "b_i": f(b_i), "b_f": f(b_f), "w_pool": f(w_pool[0]), "pool_scale": f(pool_scale),
        "gn_w": f(gn_w), "ln1_g": f(ln1_g), "ln1_b": f(ln1_b), "ln2_g": f(ln2_g), "ln2_b": f(ln2_b),
    }
    x_prompt = np.asarray(x_prompt, dtype=np.float32)
    x_sample = np.asarray(x_sample, dtype=np.float32)
    in_maps = []
    for i in range(n):
        m = dict(shared)
        m["xp"] = f(x_prompt[NPS * i:NPS * (i + 1)])
        m["xs"] = f(x_sample[NSS * i:NSS * (i + 1)])
        m["call"] = f(np.concatenate([np.asarray(c_prompt)[NPS * i:NPS * (i + 1)],
                                      np.asarray(c_sample)[NSS * i:NSS * (i + 1)]], axis=0))
        m["spool"] = f(np.asarray(state_pool)[0, NSS * i:NSS * (i + 1)])
        m["sC"] = f(np.asarray(state_mlstm_C)[0, NSS * i:NSS * (i + 1)])
        m["sn"] = f(np.asarray(state_mlstm_n)[0, NSS * i:NSS * (i + 1)])
        m["sm"] = f(np.asarray(state_mlstm_m)[0, NSS * i:NSS * (i + 1)])
        in_maps.append(m)
    nc = build_program()
    res = run_bass_kernel_spmd(nc, in_maps, core_ids=list(range(n)))
    R = res.results
    cat = lambda k: np.concatenate([np.asarray(r[k], dtype=np.float32) for r in R], axis=0)
    if DEBUG:
        kernel.debug = {k: np.asarray(R[0]["dbg_" + k]) for k in DEBUG if ("dbg_" + k) in R[0]}
    return (cat("yp"), cat("ys"), cat("pool_p")[None], cat("C_p")[None], cat("n_p")[None], cat("m_p")[None],
            cat("pool_s")[None], cat("C_s")[None], cat("n_s")[None], cat("m_s")[None])
```
